# Optimizing a Trainium2 kernel written in Bass

```python
import jax, jax.numpy as jnp
from jax import lax
import numpy as np

D_MODEL = 1024
BATCH = 8
SEQ = 4096
DEPTH = 2

GRID_W = 64
CTX_LEN = 256
N_MIXERS = 2
EPS = 1e-6
CHUNK = 128
A_WIDTH = 2 * D_MODEL
A_GROUPS = 8
A_GROUP_DIM = A_WIDTH // A_GROUPS
HEAD_DIM = 64
N_HEADS = D_MODEL // HEAD_DIM
N_KV_HEADS = 4
Q_PER_KV = N_HEADS // N_KV_HEADS
WINDOW = 128
BLOCK = 128
ROPE_THETA = 10000.0
N_FREQ = HEAD_DIM // 4
QKV_DIM = (N_HEADS + 2 * N_KV_HEADS) * HEAD_DIM
FFN_DIM = 2816
CONV_W = 3
N_LAYERS_A = (DEPTH + 1) // 2
N_LAYERS_B = DEPTH // 2

kernel_name = "hybrid_gmlp_swa_convglu_dit"


def rms_norm(x, g):
    xf = x.astype(jnp.float32)
    y = xf * lax.rsqrt(jnp.mean(xf * xf, axis=-1, keepdims=True) + EPS)
    return (y * g.astype(jnp.float32)).astype(x.dtype)


def layer_norm(x, g, b):
    xf = x.astype(jnp.float32)
    mu = jnp.mean(xf, axis=-1, keepdims=True)
    var = jnp.mean(jnp.square(xf - mu), axis=-1, keepdims=True)
    y = (xf - mu) * lax.rsqrt(var + EPS)
    return (y * g.astype(jnp.float32) + b.astype(jnp.float32)).astype(x.dtype)


def modulate(h, shift, scale):
    return h * (1 + scale) + shift


def chunk_mlp(h, w_in, ln_g, ln_b, w_s, b_s, w_out):
    bsz, n_tok, _ = h.shape
    z = jax.nn.gelu(h @ w_in)
    u, v = jnp.split(z, 2, axis=-1)
    v = layer_norm(v, ln_g, ln_b)
    v = v.reshape(bsz, n_tok // CHUNK, CHUNK, A_GROUPS, A_GROUP_DIM)
    s = jnp.einsum('gpq,bnqgc->bnpgc', w_s, v) + b_s.T[None, None, :, :, None]
    return (u * s.reshape(bsz, n_tok, A_WIDTH)) @ w_out


def axial_rope_tables(n_tok):
    ROWS = n_tok // GRID_W
    row = jnp.repeat(jnp.arange(ROWS), GRID_W).astype(jnp.float32)
    col = jnp.tile(jnp.arange(GRID_W), ROWS).astype(jnp.float32)
    inv_freq = ROPE_THETA ** (-(jnp.arange(N_FREQ, dtype=jnp.float32) / N_FREQ))
    ang = jnp.concatenate([row[:, None] * inv_freq, col[:, None] * inv_freq], axis=-1)
    return jnp.cos(ang), jnp.sin(ang)


def apply_rope(x, cos, sin):
    shape = (1, cos.shape[0]) + (1,) * (x.ndim - 3) + (cos.shape[-1],)
    c = cos.reshape(shape).astype(x.dtype)
    s = sin.reshape(shape).astype(x.dtype)
    x1, x2 = jnp.split(x, 2, axis=-1)
    return jnp.concatenate([x1 * c - x2 * s, x2 * c + x1 * s], axis=-1)


def qkv_heads(h, w_qkv, b_qkv):
    bsz, n_tok, _ = h.shape
    qkv = h @ w_qkv + b_qkv
    q, k, v = jnp.split(qkv, [N_HEADS * HEAD_DIM, (N_HEADS + N_KV_HEADS) * HEAD_DIM], axis=-1)
    q = q.reshape(bsz, n_tok, N_KV_HEADS, Q_PER_KV, HEAD_DIM)
    k = k.reshape(bsz, n_tok, N_KV_HEADS, HEAD_DIM)
    v = v.reshape(bsz, n_tok, N_KV_HEADS, HEAD_DIM)
    return q, k, v


def softmax_with_sink(logits, sink):
    sink_col = jnp.broadcast_to(sink.astype(jnp.float32)[None, :, :, None, None], logits.shape[:-1] + (1,))
    p = jax.nn.softmax(jnp.concatenate([logits, sink_col], axis=-1), axis=-1)
    return p[..., :-1]


def window_attention(h, hc, w_qkv, b_qkv, sink, w_o, cos, sin, with_ctx_out):
    bsz, n_tok, _ = h.shape
    scale = HEAD_DIM ** -0.5
    sink = sink.reshape(N_KV_HEADS, Q_PER_KV)
    q, k, v = qkv_heads(h, w_qkv, b_qkv)
    qc, kc, vc = qkv_heads(hc, w_qkv, b_qkv)
    q = apply_rope(q, cos, sin)
    k = apply_rope(k, cos, sin)

    n_blk = n_tok // BLOCK
    band = 3 * BLOCK
    pad = ((0, 0), (BLOCK, BLOCK), (0, 0), (0, 0))
    kp = jnp.pad(k, pad)
    vp = jnp.pad(v, pad)
    qb = jnp.moveaxis(q.reshape(bsz, n_blk, BLOCK, N_KV_HEADS, Q_PER_KV, HEAD_DIM), 1, 0)
    rel = jnp.arange(BLOCK)[:, None] - jnp.arange(band)[None, :] + BLOCK
    in_window = jnp.abs(rel) <= WINDOW

    def one_block(args):
        j, qj = args
        kj = lax.dynamic_slice_in_dim(kp, j * BLOCK, band, axis=1)
        vj = lax.dynamic_slice_in_dim(vp, j * BLOCK, band, axis=1)
        kpos = j * BLOCK - BLOCK + jnp.arange(band)
        valid = in_window & ((kpos >= 0) & (kpos < n_tok))[None, :]
        s_lat = jnp.einsum('bqhgd,bkhd->bhgqk', qj, kj).astype(jnp.float32) * scale
        s_lat = jnp.where(valid, s_lat, -jnp.inf)
        s_ctx = jnp.einsum('bqhgd,bkhd->bhgqk', qj, kc).astype(jnp.float32) * scale
        p = softmax_with_sink(jnp.concatenate([s_lat, s_ctx], axis=-1), sink)
        p_lat = p[..., :band].astype(vj.dtype)
        p_ctx = p[..., band:].astype(vc.dtype)
        return (jnp.einsum('bhgqk,bkhd->bqhgd', p_lat, vj)
                + jnp.einsum('bhgqk,bkhd->bqhgd', p_ctx, vc))

    out = lax.map(one_block, (jnp.arange(n_blk), qb))
    y = jnp.moveaxis(out, 0, 1).reshape(bsz, n_tok, N_HEADS * HEAD_DIM) @ w_o

    if not with_ctx_out:
        return y, None
    s_cc = jnp.einsum('bqhgd,bkhd->bhgqk', qc, kc).astype(jnp.float32) * scale
    p_cc = softmax_with_sink(s_cc, sink).astype(vc.dtype)
    yc = jnp.einsum('bhgqk,bkhd->bqhgd', p_cc, vc).reshape(bsz, hc.shape[1], N_HEADS * HEAD_DIM) @ w_o
    return y, yc


def conv_ffn(h, w_up, conv_w, conv_b, w_down):
    a, b = jnp.split(h @ w_up, 2, axis=-1)
    ap = jnp.pad(a, ((0, 0), (1, 1), (0, 0)))
    a = ap[:, :-2] * conv_w[0] + ap[:, 1:-1] * conv_w[1] + ap[:, 2:] * conv_w[2] + conv_b
    return (jax.nn.gelu(a) * b) @ w_down


def setup_inputs(seed: int = 0) -> dict:
    key = jax.random.key(seed)
    ks = jax.random.split(key, 24)
    n = lambda k, shape: jax.random.normal(k, shape, jnp.float32)
    D, E, F = D_MODEL, A_WIDTH, FFN_DIM
    return {
        "x": n(ks[0], (BATCH, SEQ, D)),
        "c": n(ks[1], (BATCH, D)),
        "ctx": n(ks[2], (BATCH, CTX_LEN, D)),
        "c_ctx": n(ks[3], (D,)),
        "ada_w": n(ks[4], (DEPTH, D, 6 * D)) * (0.5 * D ** -0.5),
        "ada_b": n(ks[5], (DEPTH, 6 * D)) * 0.01,
        "norm_g": 1.0 + 0.05 * n(ks[6], (DEPTH, 4, D)),
        "a_w_in": n(ks[7], (N_LAYERS_A, D, 2 * E)) * D ** -0.5,
        "a_ln_g": 1.0 + 0.05 * n(ks[8], (N_LAYERS_A, E)),
        "a_ln_b": 0.02 * n(ks[9], (N_LAYERS_A, E)),
        "a_w_s": n(ks[10], (N_LAYERS_A, A_GROUPS, CHUNK, CHUNK)) * CHUNK ** -0.5,
        "a_b_s": 1.0 + 0.05 * n(ks[11], (N_LAYERS_A, A_GROUPS, CHUNK)),
        "a_w_out": n(ks[12], (N_LAYERS_A, E, D)) * E ** -0.5,
        "b_w_qkv": n(ks[13], (N_LAYERS_B, D, QKV_DIM)) * D ** -0.5,
        "b_b_qkv": 0.02 * n(ks[14], (N_LAYERS_B, QKV_DIM)),
        "b_sink": n(ks[15], (N_LAYERS_B, N_HEADS)),
        "b_w_o": n(ks[16], (N_LAYERS_B, N_HEADS * HEAD_DIM, D)) * (N_HEADS * HEAD_DIM) ** -0.5,
        "f_w_up": n(ks[17], (DEPTH, D, 2 * F)) * D ** -0.5,
        "f_conv_w": n(ks[18], (DEPTH, CONV_W, F)) * CONV_W ** -0.5,
        "f_conv_b": 0.02 * n(ks[19], (DEPTH, F)),
        "f_w_down": n(ks[20], (DEPTH, F, D)) * F ** -0.5,
    }


def reference(x, c, ctx, c_ctx, ada_w, ada_b, norm_g, a_w_in, a_ln_g, a_ln_b, a_w_s, a_b_s,
              a_w_out, b_w_qkv, b_b_qkv, b_sink, b_w_o, f_w_up, f_conv_w, f_conv_b, f_w_down):
    n_tok = x.shape[1]
    cos, sin = axial_rope_tables(n_tok)
    silu_c = jax.nn.silu(c)
    silu_cc = jax.nn.silu(c_ctx)
    for i in range(DEPTH):
        last = i == DEPTH - 1
        mod_lat = (silu_c @ ada_w[i] + ada_b[i])[:, None, :]
        mod_ctx = (silu_cc @ ada_w[i] + ada_b[i])[None, None, :]
        sh1, sc1, g1, sh2, sc2, g2 = jnp.split(mod_lat, 6, axis=-1)
        csh1, csc1, cg1, csh2, csc2, cg2 = jnp.split(mod_ctx, 6, axis=-1)

        h = modulate(rms_norm(x, norm_g[i, 0]), sh1, sc1)
        hc = modulate(rms_norm(ctx, norm_g[i, 0]), csh1, csc1)
        j = i // N_MIXERS
        if i % N_MIXERS == 0:
            y = chunk_mlp(h, a_w_in[j], a_ln_g[j], a_ln_b[j], a_w_s[j], a_b_s[j], a_w_out[j])
            yc = None if last else chunk_mlp(hc, a_w_in[j], a_ln_g[j], a_ln_b[j], a_w_s[j], a_b_s[j], a_w_out[j])
        else:
            y, yc = window_attention(h, hc, b_w_qkv[j], b_b_qkv[j], b_sink[j], b_w_o[j], cos, sin, not last)
        x = x + g1 * rms_norm(y, norm_g[i, 1])
        if not last:
            ctx = ctx + cg1 * rms_norm(yc, norm_g[i, 1])

        h = modulate(rms_norm(x, norm_g[i, 2]), sh2, sc2)
        x = x + g2 * rms_norm(conv_ffn(h, f_w_up[i], f_conv_w[i], f_conv_b[i], f_w_down[i]), norm_g[i, 3])
        if not last:
            hc = modulate(rms_norm(ctx, norm_g[i, 2]), csh2, csc2)
            ctx = ctx + cg2 * rms_norm(conv_ffn(hc, f_w_up[i], f_conv_w[i], f_conv_b[i], f_w_down[i]), norm_g[i, 3])
    return x
```

```python
import numpy as np
from contextlib import ExitStack
import concourse.bass as bass
import concourse.mybir as mybir
from concourse.bass_utils import run_bass_kernel_spmd

F32 = mybir.dt.float32
BF16 = mybir.dt.bfloat16
AF = mybir.ActivationFunctionType
ALU = mybir.AluOpType

D = 1024
KC = 8
L = 4096
LC = 256
F = 2816
FC = 22
E = 2048
EC = 16
EPS = 1e-6
NCORES = 8
NBLK = L // 128
RING = 6


class Sem:
    def __init__(self, h):
        self.h = h


class Eng:
    def __init__(self, name, sem):
        self.name = name
        self.sem = sem
        self.count = 0
        self.seen = {}
        self.prog = []
        self.dsems = []
        self.dk = 0


class Tr:
    __slots__ = ("w", "r", "name")

    def __init__(self, name=""):
        self.w = None
        self.r = {}
        self.name = name


def trs(name, n):
    return [Tr("%s%d" % (name, i)) for i in range(n)]


class _Rec:
    def __init__(self):
        self.call = None

    def __getattr__(self, name):
        def f(*a, **k):
            self.call = (name, a, k)
            return self
        return f


class KB:
    def __init__(self, nc, stack):
        self.nc = nc
        self.stack = stack
        self.eng = {}
        for n in ("pe", "act", "dve", "pool", "sp"):
            self.eng[n] = Eng(n, Sem(stack.enter_context(nc.semaphore("s_" + n))))
        for n, k in (("sp", 12), ("pool", 8)):
            self.eng[n].dsems = [Sem(stack.enter_context(nc.semaphore("d_%s%d" % (n, i)))) for i in range(k)]
        self.dvals = {}

    def sb(self, name, shape, dtype):
        return self.stack.enter_context(self.nc.sbuf_tensor(name, list(shape), dtype))

    def ps(self, name, shape=(128, 512), dtype=F32):
        return self.stack.enter_context(self.nc.psum_tensor(name, list(shape), dtype))

    def _deps(self, reads, writes):
        deps = {}
        for t in reads:
            if t.w is not None:
                s, v = t.w
                if deps.get(s, 0) < v:
                    deps[s] = v
        for t in writes:
            if t.w is not None:
                s, v = t.w
                if deps.get(s, 0) < v:
                    deps[s] = v
            for s, v in t.r.items():
                if deps.get(s, 0) < v:
                    deps[s] = v
        return deps

    def _waits(self, eng, deps):
        waits = []
        for s, v in deps.items():
            if s is eng.sem and eng.name == "pe":
                continue
            if eng.seen.get(s, 0) >= v:
                continue
            eng.seen[s] = v
            waits.append((s, v))
        return waits

    def _commit(self, ev, reads, writes):
        s, v = ev
        for t in reads:
            if t.r.get(s, 0) < v:
                t.r[s] = v
        for t in writes:
            t.w = ev
            t.r = {}

    def op(self, en, emit, reads=(), writes=(), signal=True):
        eng = self.eng[en]
        waits = self._waits(eng, self._deps(reads, writes))
        if signal:
            eng.count += 1
            ev = (eng.sem, eng.count)
        else:
            ev = (eng.sem, eng.count + 1)
        semh = eng.sem.h
        rec = _Rec()
        emit(rec)
        call = rec.call

        def run(e, waits=waits, call=call, signal=signal, semh=semh):
            for s, v in waits:
                e.wait_ge(s.h, v)
            ins = getattr(e, call[0])(*call[1], **call[2])
            if signal:
                ins.then_inc(semh, 1)

        eng.prog.append(run)
        self._commit(ev, reads, writes)
        return ev

    def dma(self, qn, out, in_, reads=(), writes=(), own_sem=False):
        q = self.eng[qn]
        deps = self._deps(reads, writes)
        if own_sem:
            sem = Sem(self.stack.enter_context(self.nc.semaphore("dx%d" % len(self.dvals))))
            gen = 0
        else:
            k = q.dk
            q.dk += 1
            ns = len(q.dsems)
            sem = q.dsems[k % ns]
            gen = k // ns
        if gen > 0 and deps.get(sem, 0) < 16 * gen:
            deps[sem] = 16 * gen
        waits = self._waits(q, deps)
        ev = (sem, 16 * (gen + 1))
        self.dvals[sem] = 16 * (gen + 1)

        def run(e, waits=waits, out=out, in_=in_, sem=sem):
            for s, v in waits:
                e.wait_ge(s.h, v)
            e.dma_start(out=out, in_=in_).then_inc(sem.h, 16)

        q.prog.append(run)
        self._commit(ev, reads, writes)
        return ev

    @staticmethod
    def link(src, dst):
        for d_ in dst:
            for s_ in src:
                if s_.w is not None:
                    sm, v = s_.w
                    if d_.r.get(sm, 0) < v:
                        d_.r[sm] = v
                for sm, v in s_.r.items():
                    if d_.r.get(sm, 0) < v:
                        d_.r[sm] = v

    def barrier(self):
        allv = {e.sem: e.count for e in self.eng.values() if e.count > 0}
        allv.update(self.dvals)
        for eng in self.eng.values():
            waits = self._waits(eng, dict(allv))

            def run(e, waits=waits):
                for s, v in waits:
                    e.wait_ge(s.h, v)

            eng.prog.append(run)

    def emit_all(self):
        nc = self.nc
        with nc.Block() as block:
            @block.tensor
            def _(e):
                for f in self.eng["pe"].prog:
                    f(e)

            @block.scalar
            def _(e):
                for f in self.eng["act"].prog:
                    f(e)

            @block.vector
            def _(e):
                for f in self.eng["dve"].prog:
                    f(e)

            @block.gpsimd
            def _(e):
                for f in self.eng["pool"].prog:
                    f(e)

            @block.sync
            def _(e):
                for f in self.eng["sp"].prog:
                    f(e)


def seq_tiles(n_tok, tmax):
    out = []
    t = 0
    ntile = -(-n_tok // (tmax - 2))
    step = -(-n_tok // ntile)
    while t < n_tok:
        n = min(step, n_tok - t)
        out.append((t, n))
        t += n
    return out


ARENA_BYTES = 76 * 1024


class Prog:
    def __init__(self, phases, t_ffn=None, t_mix=256, max_tiles=None):
        self.phases = phases
        self.max_tiles = max_tiles
        self.TF = t_ffn
        self.TF_L = [384, 416] if t_ffn is None else [t_ffn, t_ffn]
        self.TMA = 384
        self.TM = t_mix
        self.nc = bass.Bass("TRN2", target_bir_lowering=False)
        self.stack = ExitStack()
        self.out_evs = []
        self.cnt = {}
        self.cast_done = {}

    def declare(self):
        nc = self.nc
        dt = lambda name, shape, kind="ExternalInput": nc.dram_tensor(name, list(shape), F32, kind=kind).ap()
        self.d_x = dt("xT", [128, KC, L])
        self.d_ctx = dt("ctxT", [128, KC, LC])
        self.d_cc = dt("cc", [128, KC, 2])
        self.d_adaw = dt("adaw", [2, 12, 128, KC, 512])
        self.d_id2 = dt("id2", [2, 2])
        self.d_adab = dt("adab", [128, 2, 48])
        self.d_ng = dt("ng", [128, 2, 4, KC])
        self.d_wup = dt("wup", [2, FC, 128, 2 * KC * 128])
        self.d_wdn = dt("wdn", [2, KC, 128, FC * 128])
        self.d_fcw = dt("fcw", [128, 2, 4, FC])
        self.d_win = dt("win", [EC, 128, 2 * KC * 128])
        self.d_wout = dt("wout", [KC, 128, EC * 128])
        self.d_wst = dt("wst", [128, 8 * 128])
        self.d_bs = dt("bs", [1, 8 * 128])
        self.d_lngb = dt("lngb", [128, 2, EC])
        self.d_ident = dt("ident", [128, 128])
        self.d_wqk = dt("wqk", [10, 128, KC * 128])
        self.d_wv = dt("wv", [128, KC * 256])
        self.d_bqk = dt("bqk", [128, 10])
        self.d_bv = dt("bv", [128, 256])
        self.d_wo = dt("wo", [KC, 128, KC * 128])
        self.d_sink = dt("sink", [128, 16])
        self.d_cos = dt("ropec", [128, L])
        self.d_sin = dt("ropes", [128, L])
        self.d_mask = dt("masks", [128, 2 * 128])
        self.d_out = dt("outT", [128, KC, L], kind="ExternalOutput")
        self.casts = []
        for nm in ("win", "wout", "wup", "wdn", "wqk", "wv", "wo"):
            src = getattr(self, "d_" + nm)
            shape = list(src.shape)
            scr = nc.dram_tensor("c_" + nm, shape, BF16, kind="Internal")
            nel = 1
            for d_ in shape:
                nel *= d_
            assert nel % 16384 == 0
            self.casts.append((nm, src.tensor, scr, nel // 16384))
            setattr(self, "d_" + nm, scr.ap())
            setattr(self, "tr_" + nm, Tr("c_" + nm))

    def phase(self, name):
        self.kb.barrier()
        self.aoff = 0
        self.atop = ARENA_BYTES

    def aalloc(self, shape, dtype, top=False):
        esz = 4 if dtype == F32 else 2
        nel = 1
        for s in shape[1:]:
            nel *= s
        nb = nel * esz
        nb_al = (nb + 63) // 64 * 64
        if top:
            self.atop -= nb_al
            off = self.atop
        else:
            off = self.aoff
            self.aoff += nb_al
        assert self.aoff <= self.atop, "arena overflow: %d > %d" % (self.aoff, self.atop)
        ap = self.ARENA[:, off // 2:(off + nb) // 2]
        if dtype == F32:
            ap = ap.bitcast(F32)
        if len(shape) > 2:
            names = ["d%d" % i for i in range(len(shape) - 1)]
            pat = "p (%s) -> p %s" % (" ".join(names), " ".join(names))
            kw = {names[i]: shape[1 + i] for i in range(1, len(names))}
            ap = ap.rearrange(pat, **kw)
        return ap

    def aview(self, off, shape, dtype):
        esz = 4 if dtype == F32 else 2
        nel = 1
        for s_ in shape[1:]:
            nel *= s_
        ap = self.ARENA[:, off // 2:(off + nel * esz) // 2]
        if dtype == F32:
            ap = ap.bitcast(F32)
        if len(shape) > 2:
            names = ["d%d" % i for i in range(len(shape) - 1)]
            pat = "p (%s) -> p %s" % (" ".join(names), " ".join(names))
            kw = {names[i]: shape[1 + i] for i in range(1, len(names))}
            ap = ap.rearrange(pat, **kw)
        return ap

    def alloc_ct(self):
        self.CT = self.aalloc([128, KC, LC + 2], F32)

    def alloc_common(self, T):
        self.HT = self.aalloc([128, KC, T], BF16)
        self.httr = Tr("ht")
        self.SQ = [self.aalloc([128, T], BF16) for i in range(3)]
        self.sqtr = trs("sq", 3)
        self.T1 = [self.aalloc([128, T], F32) for i in range(2)]
        self.t1tr = trs("t1", 2)
        self.STMP = self.aalloc([128, T], F32)
        self.stmptr = Tr("stmp")
        self.RSTD = self.aalloc([128, T], F32)
        self.rstdtr = Tr("rstd")
        self.Y = self.aalloc([128, KC, T], F32)
        self.ytr = trs("y", KC)

    def build(self):
        nc = self.nc
        with self.stack:
            self.declare()
            kb = self.kb = KB(nc, self.stack)
            self.alloc_persistent()
            self.prologue()
            ph = self.phases
            if "mix0" in ph:
                self.mixer_a(0)
            if "ffn0" in ph:
                self.ffn(0)
            if "mix1" in ph:
                self.mixer_b(1)
            if "ffn1" in ph:
                self.ffn(1)
            self.epilogue()
            kb.emit_all()
        return nc

    def alloc_persistent(self):
        kb = self.kb
        self.XT = kb.sb("XT", [128, KC, L + 2], F32)
        self.xtr = trs("x", NBLK)
        self.xpad = Tr("xpad")
        self.ctr = Tr("ct")
        self.ONES = kb.sb("ONES", [128, 128], BF16)
        self.EPSC = kb.sb("EPSC", [128, 1], F32)
        self.ctr_const = Tr("const")
        self.CC = kb.sb("CC", [128, KC, 2], F32)
        self.SC = kb.sb("SC", [128, KC, 2], F32)
        self.ADAB = kb.sb("ADAB", [128, 2, 48], F32)
        self.NG = kb.sb("NG", [128, 2, 4, KC], F32)
        self.FCW = kb.sb("FCW", [128, 2, 4, FC], F32)
        self.LNGB = kb.sb("LNGB", [128, 2, EC], F32)
        self.BQK = kb.sb("BQK", [128, 10], F32)
        self.SE = kb.sb("SE", [128, 16], F32)
        self.MOD = kb.sb("MOD", [128, 2, 2, 48], F32)
        self.DV = kb.sb("DV", [128, 2, 2, 4, KC], F32)
        self.HSAVE = kb.sb("HSAVE", [128, KC, 1], BF16)
        self.hsavetr = Tr("hsave")
        self.tr_small = Tr("small")
        self.tr_mod = [Tr("mod0"), Tr("mod1")]
        self.PS = [kb.ps("ps%d" % i) for i in range(8)]
        self.pst = trs("ps", 8)
        self.ARENA = kb.sb("ARENA", [128, ARENA_BYTES // 2], BF16)

    def rot(self, key, n):
        v = self.cnt.get(key, 0)
        self.cnt[key] = v + 1
        return v % n

    def xt_tracks(self, c0, c1):
        tl = []
        if c0 <= 0 or c1 >= L + 2:
            tl.append(self.xpad)
        t0 = max(c0 - 1, 0)
        t1 = min(c1 - 1, L)
        for b in range(t0 // 128, (t1 + 127) // 128):
            tl.append(self.xtr[b])
        return tl

    def prologue(self):
        kb = self.kb
        XT = self.XT
        self.aoff = 0
        self.atop = ARENA_BYTES
        self.alloc_ct()
        CT = self.CT
        kb.op("pool", lambda e: e.memset(self.ONES[:], 1.0), writes=[self.ctr_const])
        kb.op("pool", lambda e: e.memset(self.EPSC[:], EPS), writes=[self.ctr_const])
        kb.op("pool", lambda e: e.memset(XT[:, :, 0:1], 0.0), writes=[self.xpad])
        kb.op("pool", lambda e: e.memset(XT[:, :, L + 1:L + 2], 0.0), writes=[self.xpad])
        kb.op("pool", lambda e: e.memset(CT[:, :, 0:1], 0.0), writes=[self.ctr])
        kb.op("pool", lambda e: e.memset(CT[:, :, LC + 1:LC + 2], 0.0), writes=[self.ctr])
        self.issue_casts(("win", "wout"))
        for dst, src in ((self.CC, self.d_cc), (self.ADAB, self.d_adab), (self.NG, self.d_ng), (self.FCW, self.d_fcw),
                         (self.LNGB, self.d_lngb), (self.BQK, self.d_bqk), (self.SE, self.d_sink)):
            kb.dma("sp", dst[:], src, writes=[self.tr_small])
        kb.dma("sp", CT[:, :, 1:LC + 1], self.d_ctx, writes=[self.ctr])
        def xload(b):
            kb.dma("sp", XT[:, :, 1 + b * 512:1 + (b + 1) * 512], self.d_x[:, :, b * 512:(b + 1) * 512],
                   writes=self.xt_tracks(1 + b * 512, 1 + (b + 1) * 512))
        for b in range(L // 512):
            xload(b)
        kb.op("act", lambda e: e.activation(out=self.SC[:], in_=self.CC[:], func=AF.Silu),
              reads=[self.tr_small], writes=[self.tr_small])
        kb.op("act", lambda e: e.activation(out=self.SE[:], in_=self.SE[:], func=AF.Exp),
              reads=[self.tr_small], writes=[self.tr_small])
        self.AW = [self.aalloc([128, KC, 512], F32) for i in range(2)]
        self.awtr = trs("aw", 2)
        self.MODROW = self.aalloc([128, 6144], F32)
        self.modrowtr = Tr("modrow")
        self.ID2 = self.aalloc([128, 2], F32)
        kb.dma("sp", self.ID2[0:2, :], self.d_id2, writes=[self.tr_small])
        for l in range(2):
            self.adaln(l)

    def issue_casts(self, names):
        kb = self.kb
        for (nm, srch, scrh, rows) in self.casts:
            if nm not in names or self.cast_done.get(nm):
                continue
            self.cast_done[nm] = True
            r0 = 0
            while r0 < rows:
                nr = min(128, rows - r0)
                pat = [[16384, nr], [1, 16384]]
                kb.dma("pool", bass.AP(scrh, r0 * 16384, pat), bass.AP(srch, r0 * 16384, pat),
                       writes=[getattr(self, "tr_" + nm)], own_sem=True)
                r0 += nr

    def adaln(self, l):
        kb = self.kb
        pb = 7
        psv = self.PS[pb][:, 0:96].rearrange("p (j w) -> p j w", w=2)
        for pc in range(12):
            s = self.rot("aw", 2)
            aw = self.AW[s]
            kb.dma("sp", aw, self.d_adaw[l, pc], writes=[self.awtr[s]])

            pm = self.rot("psA", 2)
            for kc in range(KC):
                kb.op("pe", lambda e: e.matmul(self.PS[pm][0:2, 0:512], self.SC[:, kc, :], aw[:, kc, :],
                                               start=(kc == 0), stop=(kc == KC - 1)),
                      reads=[self.awtr[s], self.tr_small], writes=[self.pst[pm]], signal=(kc == KC - 1))
            kb.op("act", lambda e: e.activation(out=self.MODROW[0:2, pc * 512:(pc + 1) * 512], in_=self.PS[pm][0:2, 0:512], func=AF.Copy),
                  reads=[self.pst[pm]], writes=[self.modrowtr])
        for j in range(48):
            kb.op("pe", lambda e: e.transpose(psv[:, j, :], self.MODROW[0:2, j * 128:(j + 1) * 128], self.ID2[0:2, 0:2]),
                  reads=[self.modrowtr, self.tr_small], writes=[self.pst[pb]], signal=(j == 47))
        MOD, DV, NG = self.MOD, self.DV, self.NG
        mtr = self.tr_mod[l]
        for w in range(2):
            kb.op("dve", lambda e: e.tensor_tensor(out=MOD[:, l, w, :], in0=psv[:, :, w], in1=self.ADAB[:, l, :], op=ALU.add),
                  reads=[self.pst[pb], self.tr_small], writes=[mtr])
            kb.op("dve", lambda e: e.scalar_tensor_tensor(out=DV[:, l, w, 0, :], in0=MOD[:, l, w, 8:16], scalar=1.0,
                                                          in1=NG[:, l, 0, :], op0=ALU.add, op1=ALU.mult),
                  reads=[mtr, self.tr_small], writes=[mtr])
            kb.op("dve", lambda e: e.tensor_tensor(out=DV[:, l, w, 1, :], in0=MOD[:, l, w, 16:24], in1=NG[:, l, 1, :], op=ALU.mult),
                  reads=[mtr, self.tr_small], writes=[mtr])
            kb.op("dve", lambda e: e.scalar_tensor_tensor(out=DV[:, l, w, 2, :], in0=MOD[:, l, w, 32:40], scalar=1.0,
                                                          in1=NG[:, l, 2, :], op0=ALU.add, op1=ALU.mult),
                  reads=[mtr, self.tr_small], writes=[mtr])
            kb.op("dve", lambda e: e.tensor_tensor(out=DV[:, l, w, 3, :], in0=MOD[:, l, w, 40:48], in1=NG[:, l, 3, :], op=ALU.mult),
                  reads=[mtr, self.tr_small], writes=[mtr])

    def stats_finish(self, pb, n, rstd=None, rstdtr=None):
        kb = self.kb
        if pb == 6:
            self.flush_stat()
        if rstd is None:
            rstd, rstdtr = self.RSTD, self.rstdtr
        kb.op("act", lambda e: e.activation(out=self.STMP[:, 0:n], in_=self.PS[pb][:, 0:n], func=AF.Ln,
                                            scale=1.0 / D, bias=self.EPSC[:, 0:1]),
              reads=[self.pst[pb], self.ctr_const], writes=[self.stmptr])
        kb.op("act", lambda e: e.activation(out=rstd[:, 0:n], in_=self.STMP[:, 0:n], func=AF.Exp, scale=-0.5),
              reads=[self.stmptr], writes=[rstdtr])

    def prenorm_stats(self, src, strs, c0, n, pb=6, rstd=None, rstdtr=None):
        kb = self.kb
        for kc in range(KC):
            s = self.rot("sq", 3)
            kb.op("act", lambda e: e.activation(out=self.SQ[s][:, 0:n], in_=src[:, kc, c0:c0 + n], func=AF.Square),
                  reads=strs, writes=[self.sqtr[s]])
            kb.op("pe", lambda e: e.matmul(self.PS[pb][:, 0:n], self.ONES[:], self.SQ[s][:, 0:n],
                                           start=(kc == 0), stop=(kc == KC - 1)),
                  reads=[self.sqtr[s], self.ctr_const], writes=[self.pst[pb]], signal=True)
        self.stats_finish(pb, n, rstd, rstdtr)

    def prenorm_apply(self, src, strs, c0, n, l, w, which, rstd=None, rstdtr=None):
        kb = self.kb
        if rstd is None:
            rstd, rstdtr = self.RSTD, self.rstdtr
        ai = 0 if which == 0 else 2
        bj = 0 if which == 0 else 24
        for kc in range(KC):
            s = self.rot("t1", 2)
            kb.op("dve", lambda e: e.tensor_tensor(out=self.T1[s][:, 0:n], in0=src[:, kc, c0:c0 + n], in1=rstd[:, 0:n], op=ALU.mult),
                  reads=strs + [rstdtr], writes=[self.t1tr[s]])
            kb.op("act", lambda e: e.activation(out=self.HT[:, kc, 0:n], in_=self.T1[s][:, 0:n], func=AF.Identity,
                                                scale=self.DV[:, l, w, ai, kc:kc + 1], bias=self.MOD[:, l, w, bj + kc:bj + kc + 1]),
                  reads=[self.t1tr[s], self.tr_mod[l]], writes=[self.httr])

    def prenorm(self, src, strs, c0, n, l, w, which):
        self.prenorm_stats(src, strs, c0, n)
        self.prenorm_apply(src, strs, c0, n, l, w, which)

    def postnorm_residual(self, dst, dtrs, c0, n, l, w, gi):
        kb = self.kb
        self.stats_finish(6, n)
        for kc in range(KC):
            s = self.rot("t1", 2)
            kb.op("dve", lambda e: e.scalar_tensor_tensor(out=self.T1[s][:, 0:n], in0=self.Y[:, kc, 0:n],
                                                          scalar=self.DV[:, l, w, gi, kc:kc + 1], in1=self.RSTD[:, 0:n],
                                                          op0=ALU.mult, op1=ALU.mult),
                  reads=[self.ytr[kc], self.rstdtr, self.tr_mod[l]], writes=[self.t1tr[s]])
            kb.op("pool", lambda e: e.tensor_tensor(out=dst[:, kc, c0:c0 + n], in0=dst[:, kc, c0:c0 + n], in1=self.T1[s][:, 0:n], op=ALU.add),
                  reads=[self.t1tr[s]] + dtrs, writes=dtrs)

    def y_evac(self, pb, kc, n):
        kb = self.kb
        kb.op("act", lambda e: e.activation(out=self.Y[:, kc, 0:n], in_=self.PS[pb][:, 0:n], func=AF.Copy),
              reads=[self.pst[pb]], writes=[self.ytr[kc]])
        s = self.rot("sq", 3)
        kb.op("act", lambda e: e.activation(out=self.SQ[s][:, 0:n], in_=self.PS[pb][:, 0:n], func=AF.Square),
              reads=[self.pst[pb]], writes=[self.sqtr[s]])
        self.flush_stat()

        def stat():
            kb.op("pe", lambda e: e.matmul(self.PS[6][:, 0:n], self.ONES[:], self.SQ[s][:, 0:n], start=(kc == 0), stop=(kc == KC - 1)),
                  reads=[self.sqtr[s], self.ctr_const], writes=[self.pst[6]], signal=True)
        self.pending_stat = stat

    pending_stat = None

    def flush_stat(self):
        if self.pending_stat is not None:
            f = self.pending_stat
            self.pending_stat = None
            f()

    def ffn(self, l):
        kb = self.kb
        if not self.cast_done.get("wup"):
            self.issue_casts(("wup", "wdn"))
        TF = self.TF_L[l]
        self.phase("ffn%d" % l)
        with_ctx = (l == 0 and "ctx" in self.phases)
        if l == 0:
            self.alloc_ct()
        self.HT = self.aalloc([128, KC, TF], BF16)
        self.httr = Tr("ht")
        r1 = self.aoff
        self.aoff += 32 * TF
        assert self.aoff <= self.atop, "arena overflow"
        AB = [self.aview(r1 + i * 4 * TF, [128, TF], F32) for i in range(4)]
        abtr = trs("ab", 4)
        self.Y = self.aview(r1, [128, KC, TF], F32)
        self.ytr = trs("y", KC)
        self.SQ = [self.aalloc([128, TF], BF16) for i in range(3)]
        self.sqtr = trs("sq", 3)
        self.T1 = [self.aalloc([128, TF], F32) for i in range(2)]
        self.t1tr = trs("t1", 2)
        self.STMP = self.aalloc([128, TF], F32)
        self.stmptr = Tr("stmp")
        self.RSTD = self.aalloc([128, TF], F32)
        self.rstdtr = Tr("rstd")
        GT = self.aalloc([128, FC, TF], BF16)
        gttr = trs("gt", FC)
        WUP = [self.aalloc([128, 2, KC, 128], BF16) for i in range(3)]
        wuptr = trs("wup", 3)
        WDN = [self.aalloc([128, FC, 128], BF16) for i in range(2)]
        wdntr = trs("wdn", 4)
        fcw = self.FCW
        RSTDP = self.aalloc([128, TF], F32)
        rstdptr = Tr("rstdp")
        seqs = [(self.XT, L, 0)]
        if with_ctx:
            seqs.append((self.CT, LC, 1))
        alltiles = []
        for (src, n_tok, w) in seqs:
            for (t0, n) in seq_tiles(n_tok, TF)[:self.max_tiles]:
                strs_ = [self.ctr] if w == 1 else self.xt_tracks(t0, t0 + n + 2)
                alltiles.append((src, n_tok, w, t0, n, strs_))
        if True:
            for ti, (src, n_tok, w, t0, n, strs) in enumerate(alltiles):
                ctx = (w == 1)
                if l == 0 and ti == 1:
                    self.issue_casts(("wqk", "wv", "wo"))
                N = n + 2
                c0 = t0
                if ti == 0:
                    self.prenorm_stats(src, strs, c0, N, 7, RSTDP, rstdptr)
                    self.prenorm_apply(src, strs, c0, N, l, w, 1, RSTDP, rstdptr)
                if t0 == 0:
                    kb.op("pool", lambda e: e.memset(self.HT[:, :, 0:1], 0.0), writes=[self.httr])
                else:
                    kb.op("pool", lambda e: e.tensor_copy(out=self.HT[:, :, 0:1], in_=self.HSAVE[:]),
                          reads=[self.hsavetr], writes=[self.httr])
                kb.op("pool", lambda e: e.tensor_copy(out=self.HSAVE[:], in_=self.HT[:, :, N - 2:N - 1]),
                      reads=[self.httr], writes=[self.hsavetr])
                if t0 + n == n_tok:
                    kb.op("pool", lambda e: e.memset(self.HT[:, :, N - 1:N], 0.0), writes=[self.httr])
                for f in range(FC):
                    s = self.rot("wup", 3)
                    wu = WUP[s]
                    kb.dma("sp", wu.rearrange("p a k m -> p (a k m)"), self.d_wup[l, f], reads=[self.tr_wup], writes=[wuptr[s]])
                    pa = self.rot("psA", 2)
                    pbk = 2 + self.rot("psB", 2)
                    for kc in range(KC):
                        kb.op("pe", lambda e: e.matmul(self.PS[pa][:, 0:N], wu[:, 0, kc, :], self.HT[:, kc, 0:N],
                                                       start=(kc == 0), stop=(kc == KC - 1)),
                              reads=[wuptr[s], self.httr], writes=[self.pst[pa]], signal=(kc == KC - 1))
                    for kc in range(KC):
                        kb.op("pe", lambda e: e.matmul(self.PS[pbk][:, 0:n], wu[:, 1, kc, :], self.HT[:, kc, 1:N - 1],
                                                       start=(kc == 0), stop=(kc == KC - 1)),
                              reads=[wuptr[s], self.httr], writes=[self.pst[pbk]], signal=(kc == KC - 1))
                    a0 = self.rot("ab", 4)
                    a1 = self.rot("ab", 4)
                    kb.op("act", lambda e: e.activation(out=AB[a0][:, 0:n], in_=self.PS[pa][:, 1:N - 1], func=AF.Identity,
                                                        scale=fcw[:, l, 1, f:f + 1], bias=fcw[:, l, 3, f:f + 1]),
                          reads=[self.pst[pa], self.tr_small], writes=[abtr[a0]])
                    kb.op("dve", lambda e: e.scalar_tensor_tensor(out=AB[a1][:, 0:n], in0=self.PS[pa][:, 0:n], scalar=fcw[:, l, 0, f:f + 1],
                                                                  in1=AB[a0][:, 0:n], op0=ALU.mult, op1=ALU.add),
                          reads=[self.pst[pa], abtr[a0], self.tr_small], writes=[abtr[a1]])
                    kb.op("dve", lambda e: e.scalar_tensor_tensor(out=AB[a0][:, 0:n], in0=self.PS[pa][:, 2:N], scalar=fcw[:, l, 2, f:f + 1],
                                                                  in1=AB[a1][:, 0:n], op0=ALU.mult, op1=ALU.add),
                          reads=[self.pst[pa], abtr[a1], self.tr_small], writes=[abtr[a0]])
                    kb.op("act", lambda e: e.activation(out=AB[a1][:, 0:n], in_=AB[a0][:, 0:n], func=AF.Gelu_apprx_tanh),
                          reads=[abtr[a0]], writes=[abtr[a1]])
                    kb.op("dve", lambda e: e.tensor_tensor(out=GT[:, f, 0:n], in0=AB[a1][:, 0:n], in1=self.PS[pbk][:, 0:n], op=ALU.mult),
                          reads=[abtr[a1], self.pst[pbk]], writes=[gttr[f]])
                if ti + 1 < len(alltiles):
                    (src2, _, _, t02, n2, strs2) = alltiles[ti + 1]
                    self.prenorm_stats(src2, strs2, t02, n2 + 2, 7, RSTDP, rstdptr)
                    self.prenorm_apply(src2, strs2, t02, n2 + 2, l, alltiles[ti + 1][2], 1, RSTDP, rstdptr)
                kb.link(abtr, self.ytr)
                FH = FC // 2
                for j in range(KC):
                    s = self.rot("wdn", 2)
                    wd = WDN[s]
                    wdf = wd.rearrange("p f m -> p (f m)")
                    kb.dma("sp", wdf[:, 0:FH * 128], self.d_wdn[l, j][:, 0:FH * 128], reads=[self.tr_wdn], writes=[wdntr[2 * s]])
                    kb.dma("sp", wdf[:, FH * 128:FC * 128], self.d_wdn[l, j][:, FH * 128:FC * 128], reads=[self.tr_wdn], writes=[wdntr[2 * s + 1]])
                    py = 4 + self.rot("psY", 2)
                    for f in range(FC):
                        kb.op("pe", lambda e: e.matmul(self.PS[py][:, 0:n], wd[:, f, :], GT[:, f, 0:n],
                                                       start=(f == 0), stop=(f == FC - 1)),
                              reads=[wdntr[2 * s + (0 if f < FH else 1)], gttr[f]], writes=[self.pst[py]],
                              signal=(f == FC - 1 or f == FH - 1))
                    self.y_evac(py, j, n)
                dtrs = [self.ctr] if ctx else self.xt_tracks(c0 + 1, c0 + 1 + n)
                self.postnorm_residual(src, dtrs, c0 + 1, n, l, w, 3)
                kb.link(self.ytr, abtr)
                if (not ctx) and (l == 1 or ("stop_after_ffn0" in self.phases)):
                    self.store(t0, n)

    def store(self, t0, n):
        ev = self.kb.dma("sp", self.d_out[:, :, t0:t0 + n], self.XT[:, :, 1 + t0:1 + t0 + n],
                         reads=self.xt_tracks(1 + t0, 1 + t0 + n))
        self.out_evs.append(ev)

    def epilogue(self):
        kb = self.kb
        sp = kb.eng["sp"]
        deps = {}
        for s, v in self.out_evs:
            if deps.get(s, 0) < v:
                deps[s] = v
        waits = kb._waits(sp, deps)

        def run(e, waits=waits):
            for s, v in waits:
                e.wait_ge(s.h, v)

        sp.prog.append(run)

    def mixer_a(self, l):
        kb = self.kb
        TM = self.TMA
        self.phase("mix0")
        self.alloc_ct()
        self.HT = self.aalloc([128, KC, TM], BF16)
        self.httr = Tr("ht")
        self.SQ = [self.aalloc([128, TM], BF16) for i in range(3)]
        self.sqtr = trs("sq", 3)
        self.T1 = [self.aalloc([128, TM], F32) for i in range(2)]
        self.t1tr = trs("t1", 2)
        self.STMP = self.aalloc([128, TM], F32)
        self.stmptr = Tr("stmp")
        self.RSTD = self.aalloc([128, TM], F32)
        self.rstdtr = Tr("rstd")
        UT = self.aalloc([128, EC, TM], BF16)
        uttr = trs("ut", EC)
        rv = self.aoff
        self.aoff += 32 * TM
        VT = self.aview(rv, [128, EC, TM], BF16)
        vttr = trs("vt", EC)
        self.Y = self.aview(rv, [128, KC, TM], F32)
        self.ytr = trs("y", KC)
        WUP = [self.aalloc([128, 2, KC, 128], BF16) for i in range(3)]
        whtr = trs("wh", 6)
        WOUT = [self.aalloc([128, EC, 128], BF16) for i in range(2)]
        wouttr = trs("wout", 2)
        MEAN = self.aalloc([128, TM], F32)
        meantr = Tr("mean")
        MSQ, msqtr = self.STMP, self.stmptr
        RSTD2, rstd2tr = self.RSTD, self.rstdtr
        VTOK = [self.aalloc([128, TM // 128, 128], BF16) for i in range(2)]
        vtoktr = trs("vtok", 2)
        WST = self.aalloc([128, 8, 128], BF16)
        BS = self.aalloc([128, 8, 128], BF16)
        IDENT = self.aalloc([128, 128], BF16)
        ctr2 = Tr("const2")
        kb.dma("pool", WST.rearrange("p g m -> p (g m)"), self.d_wst, writes=[ctr2])
        kb.dma("pool", BS[0:1].rearrange("p g m -> p (g m)"), self.d_bs, writes=[ctr2])
        kb.dma("pool", IDENT, self.d_ident, writes=[ctr2])
        RSTDP = self.aalloc([128, TM], F32)
        rstdptr = Tr("rstdp")
        seqs = [(self.XT, L, 0)]
        if "ctx" in self.phases:
            seqs.append((self.CT, LC, 1))
        alltiles = []
        for (src, n_tok, w) in seqs:
            for t0 in [t0 for t0 in range(0, n_tok, TM)][:self.max_tiles]:
                n_ = min(TM, n_tok - t0)
                strs_ = [self.ctr] if w == 1 else self.xt_tracks(1 + t0, 1 + t0 + n_)
                alltiles.append((src, n_tok, w, t0, n_, strs_))
        if True:
            for ti, (src, n_tok, w, t0, n, strs) in enumerate(alltiles):
                ctx = (w == 1)
                if (not ctx) and t0 == TM:
                    self.issue_casts(("wup", "wdn"))
                NCH = n // 128
                c0 = 1 + t0
                if ti == 0:
                    self.prenorm_stats(src, strs, c0, n, 6, RSTDP, rstdptr)
                    self.prenorm_apply(src, strs, c0, n, l, w, 0, RSTDP, rstdptr)
                WH = [WUP[i // 2][:, i % 2] for i in range(6)]
                pend_ln = None
                for f in range(EC):
                    s = self.rot("wh", 6)
                    kb.dma("sp", WH[s].rearrange("p k m -> p (k m)"), self.d_win[f][:, KC * 128:2 * KC * 128],
                           reads=[self.tr_win], writes=[whtr[s]])
                    pbk = 2 + self.rot("psB", 2)
                    for kc in range(KC):
                        kb.op("pe", lambda e: e.matmul(self.PS[pbk][:, 0:n], WH[s][:, kc, :], self.HT[:, kc, 0:n],
                                                       start=(kc == 0), stop=(kc == KC - 1)),
                              reads=[whtr[s], self.httr], writes=[self.pst[pbk]], signal=(kc == KC - 1))
                    kb.op("act", lambda e: e.activation(out=VT[:, f, 0:n], in_=self.PS[pbk][:, 0:n], func=AF.Gelu_apprx_tanh),
                          reads=[self.pst[pbk]], writes=[vttr[f]])
                    sq = self.rot("sq", 3)
                    kb.op("act", lambda e: e.activation(out=self.SQ[sq][:, 0:n], in_=VT[:, f, 0:n], func=AF.Square),
                          reads=[vttr[f]], writes=[self.sqtr[sq]])
                    if pend_ln is not None:
                        pend_ln()

                    def ln_stats(f=f, sq=sq):
                        kb.op("pe", lambda e: e.matmul(self.PS[6][:, 0:n], self.ONES[:], VT[:, f, 0:n], start=(f == 0), stop=(f == EC - 1)),
                              reads=[vttr[f], self.ctr_const], writes=[self.pst[6]], signal=True)
                        kb.op("pe", lambda e: e.matmul(self.PS[7][:, 0:n], self.ONES[:], self.SQ[sq][:, 0:n], start=(f == 0), stop=(f == EC - 1)),
                              reads=[self.sqtr[sq], self.ctr_const], writes=[self.pst[7]], signal=True)
                    pend_ln = ln_stats
                pend_ln()

                def stage_u(f):
                    s = self.rot("wh", 6)
                    kb.dma("sp", WH[s].rearrange("p k m -> p (k m)"), self.d_win[f][:, 0:KC * 128],
                           reads=[self.tr_win], writes=[whtr[s]])
                    pa = self.rot("psA", 2)
                    for kc in range(KC):
                        kb.op("pe", lambda e: e.matmul(self.PS[pa][:, 0:n], WH[s][:, kc, :], self.HT[:, kc, 0:n],
                                                       start=(kc == 0), stop=(kc == KC - 1)),
                              reads=[whtr[s], self.httr], writes=[self.pst[pa]], signal=(kc == KC - 1))
                    kb.op("act", lambda e: e.activation(out=UT[:, f, 0:n], in_=self.PS[pa][:, 0:n], func=AF.Gelu_apprx_tanh),
                          reads=[self.pst[pa]], writes=[uttr[f]])
                kb.op("dve", lambda e: e.tensor_scalar(out=MEAN[:, 0:n], in0=self.PS[6][:, 0:n], scalar1=1.0 / E, scalar2=None, op0=ALU.mult),
                      reads=[self.pst[6]], writes=[meantr])
                kb.op("dve", lambda e: e.tensor_tensor(out=MSQ[:, 0:n], in0=MEAN[:, 0:n], in1=MEAN[:, 0:n], op=ALU.mult),
                      reads=[meantr], writes=[msqtr])
                kb.op("dve", lambda e: e.scalar_tensor_tensor(out=MSQ[:, 0:n], in0=self.PS[7][:, 0:n], scalar=1.0 / E, in1=MSQ[:, 0:n],
                                                              op0=ALU.mult, op1=ALU.subtract),
                      reads=[self.pst[7], msqtr], writes=[msqtr])
                kb.op("act", lambda e: e.activation(out=MSQ[:, 0:n], in_=MSQ[:, 0:n], func=AF.Ln, scale=1.0, bias=self.EPSC[:, 0:1]),
                      reads=[msqtr, self.ctr_const], writes=[msqtr])
                kb.op("act", lambda e: e.activation(out=RSTD2[:, 0:n], in_=MSQ[:, 0:n], func=AF.Exp, scale=-0.5),
                      reads=[msqtr], writes=[rstd2tr])
                st = {}

                def stage_n(f):
                    t1 = self.rot("t1", 2)
                    kb.op("dve", lambda e: e.tensor_tensor(out=self.T1[t1][:, 0:n], in0=VT[:, f, 0:n], in1=MEAN[:, 0:n], op=ALU.subtract),
                          reads=[vttr[f], meantr], writes=[self.t1tr[t1]])
                    kb.op("dve", lambda e: e.scalar_tensor_tensor(out=self.T1[t1][:, 0:n], in0=self.T1[t1][:, 0:n],
                                                                  scalar=self.LNGB[:, 0, f:f + 1], in1=RSTD2[:, 0:n],
                                                                  op0=ALU.mult, op1=ALU.mult),
                          reads=[self.t1tr[t1], rstd2tr, self.tr_small], writes=[self.t1tr[t1]])
                    kb.op("act", lambda e: e.activation(out=VT[:, f, 0:n], in_=self.T1[t1][:, 0:n], func=AF.Identity,
                                                        bias=self.LNGB[:, 1, f:f + 1], scale=1.0),
                          reads=[self.t1tr[t1], self.tr_small], writes=[vttr[f]])

                def stage_t(f):
                    ptb = 4 + self.rot("psY", 2)
                    ptv = self.PS[ptb][:, :].bitcast(BF16)
                    vk = self.rot("vtok", 2)
                    st[f] = vk
                    for c in range(NCH):
                        kb.op("pe", lambda e: e.transpose(ptv[:, c * 128:(c + 1) * 128], VT[:, f, c * 128:(c + 1) * 128], IDENT),
                              reads=[vttr[f], ctr2], writes=[self.pst[ptb]], signal=(c == NCH - 1))
                    kb.op("act", lambda e: e.activation(out=VTOK[vk].rearrange("p c m -> p (c m)")[:, 0:NCH * 128], in_=ptv[:, 0:NCH * 128], func=AF.Copy),
                          reads=[self.pst[ptb]], writes=[vtoktr[vk]])

                def stage_s(f):
                    g = f // 2
                    vk = st[f]
                    pa = 2 + self.rot("psB", 2)
                    kb.op("pe", lambda e: e.matmul(self.PS[pa][:, 0:NCH * 128], self.ONES[0:1, :],
                                                   BS[0:1, g, :].unsqueeze(1).broadcast_to([1, NCH, 128]), start=True, stop=False),
                          reads=[self.ctr_const, ctr2], writes=[self.pst[pa]], signal=False)
                    for c in range(NCH):
                        kb.op("pe", lambda e: e.matmul(self.PS[pa][:, c * 128:(c + 1) * 128], VTOK[vk][:, c, :], WST[:, g, :],
                                                       start=False, stop=(c == NCH - 1)),
                              reads=[vtoktr[vk], ctr2], writes=[self.pst[pa]], signal=(c == NCH - 1))
                    kb.op("dve", lambda e: e.tensor_tensor(out=UT[:, f, 0:n], in0=UT[:, f, 0:n], in1=self.PS[pa][:, 0:n], op=ALU.mult),
                          reads=[uttr[f], self.pst[pa]], writes=[uttr[f]])

                for step in range(EC + 2):
                    if step < EC:
                        stage_u(step)
                        stage_n(step)
                    if 0 <= step - 1 < EC:
                        stage_t(step - 1)
                    if 0 <= step - 2 < EC:
                        stage_s(step - 2)
                if ti + 1 < len(alltiles):
                    (src2, _, _, t02, n2, strs2) = alltiles[ti + 1]
                    self.prenorm_stats(src2, strs2, 1 + t02, n2, 7, RSTDP, rstdptr)
                    self.prenorm_apply(src2, strs2, 1 + t02, n2, l, alltiles[ti + 1][2], 0, RSTDP, rstdptr)
                kb.link(vttr, self.ytr)
                for j in range(KC):
                    s = self.rot("wout", 2)
                    wd = WOUT[s]
                    kb.dma("sp", wd.rearrange("p f m -> p (f m)"), self.d_wout[j], reads=[self.tr_wout], writes=[wouttr[s]])
                    py = 4 + self.rot("psY", 2)
                    for f in range(EC):
                        kb.op("pe", lambda e: e.matmul(self.PS[py][:, 0:n], wd[:, f, :], UT[:, f, 0:n], start=(f == 0), stop=(f == EC - 1)),
                              reads=[wouttr[s], uttr[f]], writes=[self.pst[py]], signal=(f == EC - 1))
                    self.y_evac(py, j, n)
                dtrs = [self.ctr] if ctx else self.xt_tracks(c0, c0 + n)
                self.postnorm_residual(src, dtrs, c0, n, l, w, 1)
                kb.link(self.ytr, vttr)
                if (not ctx) and "stop_after_mix0" in self.phases:
                    self.store(t0, n)

    def mixer_b(self, l):
        kb = self.kb
        TM = self.TM
        assert TM == 256
        n = TM
        if not self.cast_done.get("wqk"):
            self.issue_casts(("wqk", "wv", "wo"))
        self.phase("attn_ctx")
        KTC = self.aalloc([128, 2, LC], BF16, top=True)
        VC = self.aalloc([128, 2, 4, 128], BF16, top=True)
        kvctr = Tr("kvc")
        self.alloc_ct()
        self.alloc_common(TM)
        WQ = [self.aalloc([128, KC, 128], BF16) for i in range(3)]
        wqtr = trs("wq", 3)
        WV = self.aalloc([128, KC, 256], BF16)
        wvtr = Tr("wv")
        BV = self.aalloc([128, 256], F32)
        bvtr = Tr("bv")
        kb.dma("sp", BV, self.d_bv, writes=[bvtr])
        kb.op("pool", lambda e: e.memset(VC[:, :, :, 64:128], 1.0), writes=[kvctr])
        self.prenorm(self.CT, [self.ctr], 1, LC, l, 1, 0)
        for c in range(2):
            s = self.rot("wq", 3)
            kb.dma("sp", WQ[s].rearrange("p k m -> p (k m)"), self.d_wqk[8 + c], reads=[self.tr_wqk], writes=[wqtr[s]])
            pa = 4 + self.rot("psY", 2)
            for kc in range(KC):
                kb.op("pe", lambda e: e.matmul(self.PS[pa][:, 0:LC], WQ[s][:, kc, :], self.HT[:, kc, 0:LC],
                                               start=(kc == 0), stop=(kc == KC - 1)),
                      reads=[wqtr[s], self.httr], writes=[self.pst[pa]], signal=(kc == KC - 1))
            kb.op("act", lambda e: e.activation(out=KTC[:, c, :], in_=self.PS[pa][:, 0:LC], func=AF.Identity,
                                                bias=self.BQK[:, 8 + c:9 + c], scale=1.0),
                  reads=[self.pst[pa], self.tr_small], writes=[kvctr])
        kb.dma("sp", WV.rearrange("p k m -> p (k m)"), self.d_wv, reads=[self.tr_wv], writes=[wvtr])
        for blk in range(2):
            pa = 4 + self.rot("psY", 2)
            for kc in range(KC):
                kb.op("pe", lambda e: e.matmul(self.PS[pa][:, 0:256], self.HT[:, kc, blk * 128:(blk + 1) * 128], WV[:, kc, :],
                                               start=(kc == 0), stop=(kc == KC - 1)),
                      reads=[wvtr, self.httr], writes=[self.pst[pa]], signal=(kc == KC - 1))
            kb.op("dve", lambda e: e.tensor_tensor(out=VC[:, blk, :, 0:64], in0=self.PS[pa][:, 0:256].rearrange("p (h d) -> p h d", h=4),
                                                   in1=BV.rearrange("p (h d) -> p h d", h=4), op=ALU.add),
                  reads=[self.pst[pa], bvtr], writes=[kvctr])

        self.phase("attn_main")
        KTC = self.aalloc([128, 2, LC], BF16, top=True)
        VC = self.aalloc([128, 2, 4, 128], BF16, top=True)
        kvctr = Tr("kvc")
        self.alloc_common(TM)
        WQ = [self.aalloc([128, KC, 128], BF16) for i in range(3)]
        wqtr = trs("wq", 3)
        WV = self.aalloc([128, KC, 256], BF16)
        wvtr = Tr("wv")
        WO = [self.aalloc([128, KC, 128], BF16) for i in range(2)]
        wotr = trs("wo", 2)
        BV = self.aalloc([128, 256], F32)
        bvtr = Tr("bv")
        MASK = self.aalloc([128, 2, 128], BF16)
        masktr = Tr("mask")
        COS = self.aalloc([128, TM], F32)
        SIN = self.aalloc([128, TM], F32)
        ropetr = Tr("rope")
        QF = [self.aalloc([128, TM], F32) for i in range(2)]
        qftr = trs("qf", 2)
        TQ2 = [self.aalloc([128, TM], F32) for i in range(2)]
        tqtr2 = trs("tq", 2)
        UQ2 = [self.aalloc([128, TM], F32) for i in range(2)]
        uqtr2 = trs("uq", 2)
        QT = [self.aalloc([128, KC, TM], BF16) for i in range(2)]
        qttr = trs("qt", 2)
        KT = self.aalloc([128, 2, RING * 128], BF16)
        kttr = trs("kt", RING)
        VR = self.aalloc([128, RING, 4, 128], BF16)
        vrtr = trs("vr", RING)
        PT = [self.aalloc([128, 512], BF16) for i in range(3)]
        pttr = trs("pt", 3)
        RC = [self.aalloc([128, 512], F32) for i in range(2)]
        rctr = trs("rc", 2)
        OT2 = [self.aalloc([128, KC, TM], BF16) for i in range(2)]
        ottr2 = [trs("ot%d_" % i, KC) for i in range(2)]
        kb.dma("sp", BV, self.d_bv, writes=[bvtr])
        kb.dma("pool", MASK.rearrange("p a q -> p (a q)"), self.d_mask, writes=[masktr])
        kb.op("pool", lambda e: e.memset(VR[:, :, :, 64:128], 1.0), writes=vrtr)

        ntiles = L // TM
        if self.max_tiles is not None:
            ntiles = min(ntiles, self.max_tiles)
        holds = {}

        def stage_a_items(i):
            t0 = i * TM
            c0 = 1 + t0
            qs = i % 2
            items = []

            def head():
                self.prenorm(self.XT, self.xt_tracks(c0, c0 + n), c0, n, l, 0, 0)
                kb.dma("sp", COS, self.d_cos[:, t0:t0 + n], writes=[ropetr])
                kb.dma("sp", SIN, self.d_sin[:, t0:t0 + n], writes=[ropetr])
            items.append(head)
            for c in (8, 9):
                items.append(lambda c=c: proj_chunk(i, qs, c))
            items.append(lambda: v_proj(i))
            for c in range(8):
                items.append(lambda c=c: proj_chunk(i, qs, c))
            return items

        def proj_chunk(i, qs, c):
            if True:
                s = self.rot("wq", 3)
                kb.dma("sp", WQ[s].rearrange("p k m -> p (k m)"), self.d_wqk[c], reads=[self.tr_wqk], writes=[wqtr[s]])
                pa = 4 + self.rot("psY", 2)
                for kc in range(KC):
                    kb.op("pe", lambda e: e.matmul(self.PS[pa][:, 0:n], WQ[s][:, kc, :], self.HT[:, kc, 0:n],
                                                   start=(kc == 0), stop=(kc == KC - 1)),
                          reads=[wqtr[s], self.httr], writes=[self.pst[pa]], signal=(kc == KC - 1))
                qf = self.rot("qf", 2)
                TQ, tqtr, UQ, uqtr = TQ2[qf], tqtr2[qf], UQ2[qf], uqtr2[qf]
                kb.op("act", lambda e: e.activation(out=QF[qf], in_=self.PS[pa][:, 0:n], func=AF.Identity,
                                                    bias=self.BQK[:, c:c + 1], scale=1.0),
                      reads=[self.pst[pa], self.tr_small], writes=[qftr[qf]])
                kb.op("pool", lambda e: e.tensor_tensor(out=TQ, in0=QF[qf], in1=COS, op=ALU.mult),
                      reads=[qftr[qf], ropetr], writes=[tqtr])
                for q4 in range(4):
                    sp_ = q4 * 32
                    dp = sp_ ^ 32
                    kb.op("dve", lambda e: e.tensor_tensor(out=UQ[dp:dp + 32, :], in0=QF[qf][sp_:sp_ + 32, :], in1=SIN[sp_:sp_ + 32, :],
                                                           op=ALU.mult),
                          reads=[qftr[qf], ropetr], writes=[uqtr])
                if c < 8:
                    kb.op("pool", lambda e: e.tensor_tensor(out=QT[qs][:, c, :], in0=TQ, in1=UQ, op=ALU.add),
                          reads=[tqtr, uqtr], writes=[qttr[qs]])
                else:
                    for blk in range(2):
                        jb = 2 * i + blk
                        sl = jb % RING
                        kb.op("pool", lambda e: e.tensor_tensor(out=KT[:, c - 8, sl * 128:(sl + 1) * 128], in0=TQ[:, blk * 128:(blk + 1) * 128],
                                                                in1=UQ[:, blk * 128:(blk + 1) * 128], op=ALU.add),
                              reads=[tqtr, uqtr], writes=[kttr[sl]])
                        holds[("k", c - 8, sl)] = jb
        def v_proj(i):
            if i == 0:
                kb.dma("sp", WV.rearrange("p k m -> p (k m)"), self.d_wv, reads=[self.tr_wv], writes=[wvtr])
            for blk in range(2):
                jb = 2 * i + blk
                sl = jb % RING
                pa = 4 + self.rot("psY", 2)
                for kc in range(KC):
                    kb.op("pe", lambda e: e.matmul(self.PS[pa][:, 0:256], self.HT[:, kc, blk * 128:(blk + 1) * 128], WV[:, kc, :],
                                                   start=(kc == 0), stop=(kc == KC - 1)),
                          reads=[wvtr, self.httr], writes=[self.pst[pa]], signal=(kc == KC - 1))
                kb.op("dve", lambda e: e.tensor_tensor(out=VR[:, sl, :, 0:64], in0=self.PS[pa][:, 0:256].rearrange("p (h d) -> p h d", h=4),
                                                       in1=BV.rearrange("p (h d) -> p h d", h=4), op=ALU.add),
                      reads=[self.pst[pa], bvtr], writes=[vrtr[sl]])
                holds[("v", sl)] = jb

        def stage_b(i, inter):
            qs = i % 2
            OT, ottr = OT2[i % 2], ottr2[i % 2]
            jobs = []
            for blk in range(2):
                jb = 2 * i + blk
                for kvh in range(4):
                    pr, half = kvh // 2, kvh % 2
                    p0 = half * 64
                    rhs_q = QT[qs][p0:p0 + 64, pr * 4:(pr + 1) * 4, blk * 128:(blk + 1) * 128]
                    chunks = []
                    if jb > 0:
                        chunks.append(("lat", jb - 1, 0))
                    chunks.append(("lat", jb, None))
                    if jb < NBLK - 1:
                        chunks.append(("lat", jb + 1, 1))
                    chunks.append(("ctx", 0, None))
                    chunks.append(("ctx", 1, None))
                    pv = 2 + self.rot("psB", 2)
                    for ci, (kind, kbk, mk) in enumerate(chunks):
                        if kind == "lat":
                            sl = kbk % RING
                            job = dict(lhs_k=KT[p0:p0 + 64, pr, sl * 128:(sl + 1) * 128], ktr=kttr[sl],
                                       lhs_v=VR[:, sl, kvh, :], vtr=vrtr[sl], need=(kbk, sl))
                        else:
                            job = dict(lhs_k=KTC[p0:p0 + 64, pr, kbk * 128:(kbk + 1) * 128], ktr=kvctr,
                                       lhs_v=VC[:, kbk, kvh, :], vtr=kvctr)
                        job.update(mk=mk, first=(ci == 0), last=(ci == len(chunks) - 1), pv=pv, rhs_q=rhs_q,
                                   kvh=kvh, blk=blk, pr=pr, p0=p0)
                        jobs.append(job)

            def emit_s(jb_):
                if "need" in jb_:
                    kbk_, sl_ = jb_["need"]
                    assert holds.get(("k", jb_["pr"], sl_)) == kbk_ and holds.get(("v", sl_)) == kbk_, "ring slot not yet produced"
                pa = (0, 1, 7)[self.rot("psS", 3)]
                ps_ = self.rot("pt", 3)
                jb_["pt"] = ps_
                kb.op("pe", lambda e: e.matmul(self.PS[pa][:, 0:512], jb_["lhs_k"], jb_["rhs_q"], start=True, stop=True),
                      reads=[jb_["ktr"], qttr[qs]], writes=[self.pst[pa]], signal=True)
                kb.op("act", lambda e: e.activation(out=PT[ps_], in_=self.PS[pa][:, 0:512], func=AF.Exp, scale=0.125),
                      reads=[self.pst[pa]], writes=[pttr[ps_]])
                if jb_["mk"] is not None:
                    kb.op("dve", lambda e: e.tensor_tensor(out=PT[ps_].rearrange("p (g q) -> p g q", g=4),
                                                           in0=PT[ps_].rearrange("p (g q) -> p g q", g=4),
                                                           in1=MASK[:, jb_["mk"], :].unsqueeze(1).broadcast_to([128, 4, 128]), op=ALU.mult),
                          reads=[pttr[ps_], masktr], writes=[pttr[ps_]])

            def emit_pv(jb_):
                pv = jb_["pv"]
                ps_ = jb_["pt"]
                kb.op("pe", lambda e: e.matmul(self.PS[pv][:, 0:512], jb_["lhs_v"], PT[ps_], start=jb_["first"], stop=jb_["last"]),
                      reads=[jb_["vtr"], pttr[ps_]], writes=[self.pst[pv]], signal=True)
                if not jb_["last"]:
                    return
                kvh, blk, pr, p0 = jb_["kvh"], jb_["blk"], jb_["pr"], jb_["p0"]
                rc = self.rot("rc", 2)
                for g in range(4):
                    h = kvh * 4 + g
                    kb.op("act", lambda e: e.activation(out=RC[rc][0:64, g * 128:(g + 1) * 128], in_=self.PS[pv][64:128, g * 128:(g + 1) * 128],
                                                        func=AF.Ln, bias=self.SE[64:128, h:h + 1], scale=1.0),
                          reads=[self.pst[pv], self.tr_small], writes=[rctr[rc]])
                kb.op("act", lambda e: e.activation(out=RC[rc][0:64, :], in_=RC[rc][0:64, :], func=AF.Exp, scale=-1.0),
                      reads=[rctr[rc]], writes=[rctr[rc]])
                kb.op("dve", lambda e: e.tensor_tensor(out=OT[p0:p0 + 64, pr * 4:(pr + 1) * 4, blk * 128:(blk + 1) * 128],
                                                       in0=self.PS[pv][0:64, 0:512].rearrange("p (g q) -> p g q", g=4),
                                                       in1=RC[rc][0:64, :].rearrange("p (g q) -> p g q", g=4), op=ALU.mult),
                      reads=[self.pst[pv], rctr[rc]], writes=ottr[pr * 4:(pr + 1) * 4])

            SK = 2
            for idx in range(len(jobs) + SK):
                if idx < len(jobs):
                    emit_s(jobs[idx])
                if 0 <= idx - SK < len(jobs):
                    emit_pv(jobs[idx - SK])
                if inter:
                    inter.pop(0)()
            while inter:
                inter.pop(0)()

        def stage_c_items(i):
            t0 = i * TM
            c0 = 1 + t0
            OT, ottr = OT2[i % 2], ottr2[i % 2]
            items = []

            def wo_group(j):
                s = self.rot("wo", 2)
                kb.dma("sp", WO[s].rearrange("p k m -> p (k m)"), self.d_wo[j], reads=[self.tr_wo], writes=[wotr[s]])
                py = 4 + self.rot("psY", 2)
                for c in range(KC):
                    kb.op("pe", lambda e: e.matmul(self.PS[py][:, 0:n], WO[s][:, c, :], OT[:, c, :], start=(c == 0), stop=(c == KC - 1)),
                          reads=[wotr[s], ottr[c]], writes=[self.pst[py]], signal=(c == KC - 1))
                self.y_evac(py, j, n)

            def post_kc(kc):
                dtrs = self.xt_tracks(c0, c0 + n)
                kb.op("dve", lambda e: e.scalar_tensor_tensor(out=self.Y[:, kc, 0:n], in0=self.Y[:, kc, 0:n],
                                                              scalar=self.DV[:, l, 0, 1, kc:kc + 1], in1=self.RSTD[:, 0:n],
                                                              op0=ALU.mult, op1=ALU.mult),
                      reads=[self.rstdtr, self.tr_mod[l]], writes=[self.ytr[kc]])
                kb.op("pool", lambda e: e.tensor_tensor(out=self.XT[:, kc, c0:c0 + n], in0=self.XT[:, kc, c0:c0 + n], in1=self.Y[:, kc, 0:n], op=ALU.add),
                      reads=[self.ytr[kc]] + dtrs, writes=dtrs)

            for j in range(KC):
                items.append(lambda j=j: wo_group(j))
            items.append(lambda: self.stats_finish(6, n))
            for kc in range(KC):
                items.append(lambda kc=kc: post_kc(kc))
            if "stop_after_mix1" in self.phases:
                items.append(lambda: self.store(t0, n))
            return items

        for it in stage_a_items(0):
            it()
        for i in range(ntiles):
            inter = stage_c_items(i - 1) if i >= 1 else []
            if i + 1 < L // TM:
                inter = inter + stage_a_items(i + 1)
            stage_b(i, inter)
        for it in stage_c_items(ntiles - 1):
            it()


def _qperm():
    cols = []
    for c in range(8):
        pr, g = c // 4, c % 4
        for hh in (2 * pr, 2 * pr + 1):
            h = hh * 4 + g
            cols.extend(range(h * 64, (h + 1) * 64))
    return np.array(cols)


def prep_shared(inp):
    f32 = np.float32
    out = {}
    ada_w = np.asarray(inp["ada_w"], f32)
    out["adaw"] = np.ascontiguousarray(ada_w.reshape(2, KC, 128, 12, 512).transpose(0, 3, 2, 1, 4))
    out["id2"] = np.eye(2, dtype=f32)
    ada_b = np.asarray(inp["ada_b"], f32)
    out["adab"] = np.ascontiguousarray(ada_b.reshape(2, 48, 128).transpose(2, 0, 1))
    ng = np.asarray(inp["norm_g"], f32)
    out["ng"] = np.ascontiguousarray(ng.reshape(2, 4, KC, 128).transpose(3, 0, 1, 2))
    wup = np.asarray(inp["f_w_up"], f32)
    out["wup"] = np.ascontiguousarray(wup.reshape(2, KC, 128, 2, FC, 128).transpose(0, 4, 2, 3, 1, 5)).reshape(2, FC, 128, 2 * KC * 128)
    wdn = np.asarray(inp["f_w_down"], f32)
    out["wdn"] = np.ascontiguousarray(wdn.reshape(2, FC, 128, KC, 128).transpose(0, 3, 2, 1, 4)).reshape(2, KC, 128, FC * 128)
    cw = np.asarray(inp["f_conv_w"], f32)
    cb = np.asarray(inp["f_conv_b"], f32)
    fcw = np.concatenate([cw, cb[:, None, :]], axis=1)
    out["fcw"] = np.ascontiguousarray(fcw.reshape(2, 4, FC, 128).transpose(3, 0, 1, 2))
    win = np.asarray(inp["a_w_in"], f32)[0]
    out["win"] = np.ascontiguousarray(win.reshape(KC, 128, 2, EC, 128).transpose(3, 1, 2, 0, 4)).reshape(EC, 128, 2 * KC * 128)
    wout = np.asarray(inp["a_w_out"], f32)[0]
    out["wout"] = np.ascontiguousarray(wout.reshape(EC, 128, KC, 128).transpose(2, 1, 0, 3)).reshape(KC, 128, EC * 128)
    ws = np.asarray(inp["a_w_s"], f32)[0]
    out["wst"] = np.ascontiguousarray(ws.transpose(2, 0, 1)).reshape(128, 8 * 128)
    out["bs"] = np.ascontiguousarray(np.asarray(inp["a_b_s"], f32)[0]).reshape(1, 8 * 128)
    lngb = np.stack([np.asarray(inp["a_ln_g"], f32)[0], np.asarray(inp["a_ln_b"], f32)[0]], axis=0)
    out["lngb"] = np.ascontiguousarray(lngb.reshape(2, EC, 128).transpose(2, 0, 1))
    out["ident"] = np.eye(128, dtype=f32)
    wqkv = np.asarray(inp["b_w_qkv"], f32)[0]
    bqkv = np.asarray(inp["b_b_qkv"], f32)[0]
    qp = _qperm()
    cols = np.concatenate([qp, np.arange(1024, 1280)])
    wqk = wqkv[:, cols]
    out["wqk"] = np.ascontiguousarray(wqk.reshape(KC, 128, 10, 128).transpose(2, 1, 0, 3)).reshape(10, 128, KC * 128)
    out["bqk"] = np.ascontiguousarray(bqkv[cols].reshape(10, 128).T)
    wv = wqkv[:, 1280:1536]
    out["wv"] = np.ascontiguousarray(wv.reshape(KC, 128, 256).transpose(1, 0, 2)).reshape(128, KC * 256)
    out["bv"] = np.ascontiguousarray(np.broadcast_to(bqkv[1280:1536][None, :], (128, 256)))
    wo = np.asarray(inp["b_w_o"], f32)[0][qp, :]
    out["wo"] = np.ascontiguousarray(wo.reshape(KC, 128, KC, 128).transpose(2, 1, 0, 3)).reshape(KC, 128, KC * 128)
    out["sink"] = np.ascontiguousarray(np.broadcast_to(np.asarray(inp["b_sink"], f32)[0][None, :], (128, 16)))
    t = np.arange(L)
    row = (t // 64).astype(f32)
    col = (t % 64).astype(f32)
    inv_freq = (f32(10000.0) ** (-(np.arange(16, dtype=f32) / f32(16)))).astype(f32)
    ang = np.concatenate([row[:, None] * inv_freq, col[:, None] * inv_freq], axis=-1).astype(f32)
    cosT = np.cos(ang).astype(f32).T
    sinT = np.sin(ang).astype(f32).T
    p = np.arange(128)
    out["ropec"] = np.ascontiguousarray(cosT[p % 32])
    sgn = np.where((p % 64) >= 32, -1.0, 1.0).astype(f32)
    out["ropes"] = np.ascontiguousarray(sinT[p % 32] * sgn[:, None])
    kl = np.arange(128)[:, None]
    ql = np.arange(128)[None, :]
    out["masks"] = np.ascontiguousarray(np.stack([(kl >= ql), (kl <= ql)], axis=1).astype(f32)).reshape(128, 256)
    return out


def prep_core(inp, b):
    f32 = np.float32
    x = np.asarray(inp["x"][b], f32)
    ctx = np.asarray(inp["ctx"][b], f32)
    c = np.asarray(inp["c"][b], f32)
    cctx = np.asarray(inp["c_ctx"], f32)
    m = {}
    m["xT"] = np.ascontiguousarray(x.reshape(L, KC, 128).transpose(2, 1, 0))
    m["ctxT"] = np.ascontiguousarray(ctx.reshape(LC, KC, 128).transpose(2, 1, 0))
    cc = np.stack([c, cctx], axis=-1)
    m["cc"] = np.ascontiguousarray(cc.reshape(KC, 128, 2).transpose(1, 0, 2))
    return m


def run_prog(inp, phases, cores=NCORES, **kw):
    p = Prog(phases, **kw)
    nc = p.build()
    shared = prep_shared(inp)
    in_maps = []
    for b in range(cores):
        m = prep_core(inp, b)
        m.update(shared)
        in_maps.append(m)
    res = run_bass_kernel_spmd(nc, in_maps, core_ids=list(range(cores)))
    outs = []
    for b in range(cores):
        o = res.results[b]["outT"]
        outs.append(np.ascontiguousarray(o.transpose(2, 1, 0)).reshape(L, D))
    return np.stack(outs, axis=0)


def kernel(**inputs):
    out = run_prog(inputs, {"mix0", "ffn0", "ctx", "mix1", "ffn1"})
    return out.astype(np.float32)
```

```python
import numpy as np
from contextlib import ExitStack
import concourse.bass as bass
import concourse.mybir as mybir
from concourse.bass_utils import run_bass_kernel_spmd

F32 = mybir.dt.float32
BF16 = mybir.dt.bfloat16
AF = mybir.ActivationFunctionType
ALU = mybir.AluOpType

D = 1024
KC = 8
L = 4096
LC = 256
F = 2816
FC = 22
E = 2048
EC = 16
EPS = 1e-6
NCORES = 8
NBLK = L // 128
RING = 6


class Sem:
    def __init__(self, h):
        self.h = h


class Eng:
    def __init__(self, name, sem):
        self.name = name
        self.sem = sem
        self.count = 0
        self.seen = {}
        self.prog = []
        self.dsems = []
        self.dk = 0


class Tr:
    __slots__ = ("w", "r", "name")

    def __init__(self, name=""):
        self.w = None
        self.r = {}
        self.name = name


def trs(name, n):
    return [Tr("%s%d" % (name, i)) for i in range(n)]


class _Rec:
    def __init__(self):
        self.call = None

    def __getattr__(self, name):
        def f(*a, **k):
            self.call = (name, a, k)
            return self
        return f


class KB:
    def __init__(self, nc, stack):
        self.nc = nc
        self.stack = stack
        self.eng = {}
        for n in ("pe", "act", "dve", "pool", "sp"):
            self.eng[n] = Eng(n, Sem(stack.enter_context(nc.semaphore("s_" + n))))
        for n, k in (("sp", 12), ("pool", 8)):
            self.eng[n].dsems = [Sem(stack.enter_context(nc.semaphore("d_%s%d" % (n, i)))) for i in range(k)]
        self.dvals = {}

    def sb(self, name, shape, dtype):
        return self.stack.enter_context(self.nc.sbuf_tensor(name, list(shape), dtype))

    def ps(self, name, shape=(128, 512), dtype=F32):
        return self.stack.enter_context(self.nc.psum_tensor(name, list(shape), dtype))

    def _deps(self, reads, writes):
        deps = {}
        for t in reads:
            if t.w is not None:
                s, v = t.w
                if deps.get(s, 0) < v:
                    deps[s] = v
        for t in writes:
            if t.w is not None:
                s, v = t.w
                if deps.get(s, 0) < v:
                    deps[s] = v
            for s, v in t.r.items():
                if deps.get(s, 0) < v:
                    deps[s] = v
        return deps

    def _waits(self, eng, deps):
        waits = []
        for s, v in deps.items():
            if s is eng.sem and eng.name == "pe":
                continue
            if eng.seen.get(s, 0) >= v:
                continue
            eng.seen[s] = v
            waits.append((s, v))
        return waits

    def _commit(self, ev, reads, writes):
        s, v = ev
        for t in reads:
            if t.r.get(s, 0) < v:
                t.r[s] = v
        for t in writes:
            t.w = ev
            t.r = {}

    def op(self, en, emit, reads=(), writes=(), signal=True):
        eng = self.eng[en]
        waits = self._waits(eng, self._deps(reads, writes))
        if signal:
            eng.count += 1
            ev = (eng.sem, eng.count)
        else:
            ev = (eng.sem, eng.count + 1)
        semh = eng.sem.h
        rec = _Rec()
        emit(rec)
        call = rec.call

        def run(e, waits=waits, call=call, signal=signal, semh=semh):
            for s, v in waits:
                e.wait_ge(s.h, v)
            ins = getattr(e, call[0])(*call[1], **call[2])
            if signal:
                ins.then_inc(semh, 1)

        eng.prog.append(run)
        self._commit(ev, reads, writes)
        return ev

    def dma(self, qn, out, in_, reads=(), writes=(), own_sem=False):
        q = self.eng[qn]
        deps = self._deps(reads, writes)
        if own_sem:
            sem = Sem(self.stack.enter_context(self.nc.semaphore("dx%d" % len(self.dvals))))
            gen = 0
        else:
            k = q.dk
            q.dk += 1
            ns = len(q.dsems)
            sem = q.dsems[k % ns]
            gen = k // ns
        if gen > 0 and deps.get(sem, 0) < 16 * gen:
            deps[sem] = 16 * gen
        waits = self._waits(q, deps)
        ev = (sem, 16 * (gen + 1))
        self.dvals[sem] = 16 * (gen + 1)

        def run(e, waits=waits, out=out, in_=in_, sem=sem):
            for s, v in waits:
                e.wait_ge(s.h, v)
            e.dma_start(out=out, in_=in_).then_inc(sem.h, 16)

        q.prog.append(run)
        self._commit(ev, reads, writes)
        return ev

    @staticmethod
    def link(src, dst):
        for d_ in dst:
            for s_ in src:
                if s_.w is not None:
                    sm, v = s_.w
                    if d_.r.get(sm, 0) < v:
                        d_.r[sm] = v
                for sm, v in s_.r.items():
                    if d_.r.get(sm, 0) < v:
                        d_.r[sm] = v

    def barrier(self):
        allv = {e.sem: e.count for e in self.eng.values() if e.count > 0}
        allv.update(self.dvals)
        for eng in self.eng.values():
            waits = self._waits(eng, dict(allv))

            def run(e, waits=waits):
                for s, v in waits:
                    e.wait_ge(s.h, v)

            eng.prog.append(run)

    def emit_all(self):
        nc = self.nc
        with nc.Block() as block:
            @block.tensor
            def _(e):
                for f in self.eng["pe"].prog:
                    f(e)

            @block.scalar
            def _(e):
                for f in self.eng["act"].prog:
                    f(e)

            @block.vector
            def _(e):
                for f in self.eng["dve"].prog:
                    f(e)

            @block.gpsimd
            def _(e):
                for f in self.eng["pool"].prog:
                    f(e)

            @block.sync
            def _(e):
                for f in self.eng["sp"].prog:
                    f(e)


def seq_tiles(n_tok, tmax):
    out = []
    t = 0
    ntile = -(-n_tok // (tmax - 2))
    step = -(-n_tok // ntile)
    while t < n_tok:
        n = min(step, n_tok - t)
        out.append((t, n))
        t += n
    return out


ARENA_BYTES = 78400


class Prog:
    def __init__(self, phases, t_ffn=None, t_mix=256, max_tiles=None):
        self.phases = phases
        self.max_tiles = max_tiles
        self.TF = t_ffn
        self.TF_L = [448, 464] if t_ffn is None else [t_ffn, t_ffn]
        self.TMA = 384
        self.TM = t_mix
        self.nc = bass.Bass("TRN2", target_bir_lowering=False)
        self.stack = ExitStack()
        self.out_evs = []
        self.cnt = {}
        self.cast_done = {}

    def declare(self):
        nc = self.nc
        dt = lambda name, shape, kind="ExternalInput": nc.dram_tensor(name, list(shape), F32, kind=kind).ap()
        self.d_x = dt("xT", [128, KC, L])
        self.d_ctx = dt("ctxT", [128, KC, LC])
        self.d_cc = dt("cc", [128, KC, 2])
        self.d_adaw = dt("adaw", [2, 12, 128, KC, 512])
        self.d_id2 = dt("id2", [2, 2])
        self.d_adab = dt("adab", [128, 2, 48])
        self.d_ng = dt("ng", [128, 2, 4, KC])
        self.d_wup = dt("wup", [2, FC, 128, 2 * KC * 128])
        self.d_wdn = dt("wdn", [2, KC, 128, FC * 128])
        self.d_fcw = dt("fcw", [128, 2, 4, FC])
        self.d_win = dt("win", [EC, 128, 2 * KC * 128])
        self.d_wout = dt("wout", [KC, 128, EC * 128])
        self.d_wst = dt("wst", [128, 8 * 128])
        self.d_bs = dt("bs", [1, 8 * 128])
        self.d_lngb = dt("lngb", [128, 2, EC])
        self.d_ident = dt("ident", [128, 128])
        self.d_wqk = dt("wqk", [10, 128, KC * 128])
        self.d_wv = dt("wv", [128, KC * 256])
        self.d_bqk = dt("bqk", [128, 10])
        self.d_bv = dt("bv", [128, 256])
        self.d_wo = dt("wo", [KC, 128, KC * 128])
        self.d_sink = dt("sink", [128, 16])
        self.d_cos = dt("ropec", [128, L])
        self.d_sin = dt("ropes", [128, L])
        self.d_mask = dt("masks", [128, 2 * 128])
        self.d_out = dt("outT", [128, KC, L], kind="ExternalOutput")
        self.casts = []
        for nm in ("win", "wout", "wup", "wdn", "wqk", "wv", "wo"):
            src = getattr(self, "d_" + nm)
            shape = list(src.shape)
            scr = nc.dram_tensor("c_" + nm, shape, BF16, kind="Internal")
            nel = 1
            for d_ in shape:
                nel *= d_
            assert nel % 16384 == 0
            self.casts.append((nm, src.tensor, scr, nel // 16384))
            setattr(self, "d_" + nm, scr.ap())
            setattr(self, "tr_" + nm, Tr("c_" + nm))

    def phase(self, name):
        self.kb.barrier()
        self.aoff = 0
        self.atop = ARENA_BYTES

    def aalloc(self, shape, dtype, top=False):
        esz = 4 if dtype == F32 else 2
        nel = 1
        for s in shape[1:]:
            nel *= s
        nb = nel * esz
        nb_al = (nb + 63) // 64 * 64
        if top:
            self.atop -= nb_al
            off = self.atop
        else:
            off = self.aoff
            self.aoff += nb_al
        assert self.aoff <= self.atop, "arena overflow: %d > %d" % (self.aoff, self.atop)
        ap = self.ARENA[:, off // 2:(off + nb) // 2]
        if dtype == F32:
            ap = ap.bitcast(F32)
        if len(shape) > 2:
            names = ["d%d" % i for i in range(len(shape) - 1)]
            pat = "p (%s) -> p %s" % (" ".join(names), " ".join(names))
            kw = {names[i]: shape[1 + i] for i in range(1, len(names))}
            ap = ap.rearrange(pat, **kw)
        return ap

    def aview(self, off, shape, dtype):
        esz = 4 if dtype == F32 else 2
        nel = 1
        for s_ in shape[1:]:
            nel *= s_
        ap = self.ARENA[:, off // 2:(off + nel * esz) // 2]
        if dtype == F32:
            ap = ap.bitcast(F32)
        if len(shape) > 2:
            names = ["d%d" % i for i in range(len(shape) - 1)]
            pat = "p (%s) -> p %s" % (" ".join(names), " ".join(names))
            kw = {names[i]: shape[1 + i] for i in range(1, len(names))}
            ap = ap.rearrange(pat, **kw)
        return ap

    def alloc_ct(self):
        self.CT = self.aalloc([128, KC, LC + 2], F32)

    def alloc_common(self, T):
        self.HT = self.aalloc([128, KC, T], BF16)
        self.httr = Tr("ht")
        self.SQ = [self.aalloc([128, T], BF16) for i in range(3)]
        self.sqtr = trs("sq", 3)
        self.T1 = [self.aalloc([128, T], F32) for i in range(2)]
        self.t1tr = trs("t1", 2)
        self.STMP = self.aalloc([128, T], F32)
        self.stmptr = Tr("stmp")
        self.RSTD = self.aalloc([128, T], F32)
        self.rstdtr = Tr("rstd")
        self.Y = self.aalloc([128, KC, T], F32)
        self.ytr = trs("y", KC)

    def build(self):
        nc = self.nc
        with self.stack:
            self.declare()
            kb = self.kb = KB(nc, self.stack)
            self.alloc_persistent()
            self.prologue()
            ph = self.phases
            if "mix0" in ph:
                self.mixer_a(0)
            if "ffn0" in ph:
                self.ffn(0)
            if "mix1" in ph:
                self.mixer_b(1)
            if "ffn1" in ph:
                self.ffn(1)
            self.epilogue()
            kb.emit_all()
        return nc

    def alloc_persistent(self):
        kb = self.kb
        self.XT = kb.sb("XT", [128, KC, L + 2], F32)
        self.xtr = trs("x", NBLK)
        self.xpad = Tr("xpad")
        self.ctr = Tr("ct")
        self.ONES = kb.sb("ONES", [128, 128], BF16)
        self.ctr_const = Tr("const")
        self.CC = kb.sb("CC", [128, KC, 2], F32)
        self.SC = kb.sb("SC", [128, KC, 2], F32)
        self.ADAB = kb.sb("ADAB", [128, 2, 48], F32)
        self.NG = kb.sb("NG", [128, 2, 4, KC], F32)
        self.FCW = kb.sb("FCW", [128, 2, 4, FC], F32)
        self.LNGB = kb.sb("LNGB", [128, 2, EC], F32)
        self.BQK = kb.sb("BQK", [128, 16], F32)
        self.EPSC = self.BQK[:, 10:11]
        self.SE = kb.sb("SE", [128, 16], F32)
        self.MOD = kb.sb("MOD", [128, 2, 2, 48], F32)
        self.DV = kb.sb("DV", [128, 2, 2, 4, KC], F32)
        self.HSAVE = kb.sb("HSAVE", [128, KC, 1], BF16)
        self.hsavetr = Tr("hsave")
        self.tr_small = Tr("small")
        self.tr_mod = [Tr("mod0"), Tr("mod1")]
        self.PS = [kb.ps("ps%d" % i) for i in range(8)]
        self.pst = trs("ps", 8)
        self.ARENA = kb.sb("ARENA", [128, ARENA_BYTES // 2], BF16)

    def rot(self, key, n):
        v = self.cnt.get(key, 0)
        self.cnt[key] = v + 1
        return v % n

    def xt_tracks(self, c0, c1):
        tl = []
        if c0 <= 0 or c1 >= L + 2:
            tl.append(self.xpad)
        t0 = max(c0 - 1, 0)
        t1 = min(c1 - 1, L)
        for b in range(t0 // 128, (t1 + 127) // 128):
            tl.append(self.xtr[b])
        return tl

    def prologue(self):
        kb = self.kb
        XT = self.XT
        self.aoff = 0
        self.atop = ARENA_BYTES
        self.alloc_ct()
        CT = self.CT
        kb.op("pool", lambda e: e.memset(self.ONES[:], 1.0), writes=[self.ctr_const])
        kb.op("pool", lambda e: e.memset(self.EPSC[:], EPS), writes=[self.ctr_const])
        kb.op("pool", lambda e: e.memset(XT[:, :, 0:1], 0.0), writes=[self.xpad])
        kb.op("pool", lambda e: e.memset(XT[:, :, L + 1:L + 2], 0.0), writes=[self.xpad])
        kb.op("pool", lambda e: e.memset(CT[:, :, 0:1], 0.0), writes=[self.ctr])
        kb.op("pool", lambda e: e.memset(CT[:, :, LC + 1:LC + 2], 0.0), writes=[self.ctr])
        self.issue_casts(("win", "wout"))
        for dst, src in ((self.CC, self.d_cc), (self.ADAB, self.d_adab), (self.NG, self.d_ng), (self.FCW, self.d_fcw),
                         (self.LNGB, self.d_lngb), (self.BQK[:, 0:10], self.d_bqk), (self.SE, self.d_sink)):
            kb.dma("sp", dst[:], src, writes=[self.tr_small])
        kb.dma("sp", CT[:, :, 1:LC + 1], self.d_ctx, writes=[self.ctr])
        def xload(b):
            kb.dma("sp", XT[:, :, 1 + b * 512:1 + (b + 1) * 512], self.d_x[:, :, b * 512:(b + 1) * 512],
                   writes=self.xt_tracks(1 + b * 512, 1 + (b + 1) * 512))
        for b in range(L // 512):
            xload(b)
        kb.op("act", lambda e: e.activation(out=self.SC[:], in_=self.CC[:], func=AF.Silu),
              reads=[self.tr_small], writes=[self.tr_small])
        kb.op("act", lambda e: e.activation(out=self.SE[:], in_=self.SE[:], func=AF.Exp),
              reads=[self.tr_small], writes=[self.tr_small])
        self.AW = [self.aalloc([128, KC, 512], F32) for i in range(2)]
        self.awtr = trs("aw", 2)
        self.MODROW = self.aalloc([128, 6144], F32)
        self.modrowtr = Tr("modrow")
        self.ID2 = self.aalloc([128, 2], F32)
        kb.dma("sp", self.ID2[0:2, :], self.d_id2, writes=[self.tr_small])
        for l in range(2):
            self.adaln(l)

    def issue_casts(self, names):
        kb = self.kb
        for (nm, srch, scrh, rows) in self.casts:
            if nm not in names or self.cast_done.get(nm):
                continue
            self.cast_done[nm] = True
            r0 = 0
            while r0 < rows:
                nr = min(128, rows - r0)
                pat = [[16384, nr], [1, 16384]]
                kb.dma("pool", bass.AP(scrh, r0 * 16384, pat), bass.AP(srch, r0 * 16384, pat),
                       writes=[getattr(self, "tr_" + nm)], own_sem=True)
                r0 += nr

    def adaln(self, l):
        kb = self.kb
        pb = 7
        psv = self.PS[pb][:, 0:96].rearrange("p (j w) -> p j w", w=2)
        for pc in range(12):
            s = self.rot("aw", 2)
            aw = self.AW[s]
            kb.dma("sp", aw, self.d_adaw[l, pc], writes=[self.awtr[s]])

            pm = self.rot("psA", 2)
            for kc in range(KC):
                kb.op("pe", lambda e: e.matmul(self.PS[pm][0:2, 0:512], self.SC[:, kc, :], aw[:, kc, :],
                                               start=(kc == 0), stop=(kc == KC - 1)),
                      reads=[self.awtr[s], self.tr_small], writes=[self.pst[pm]], signal=(kc == KC - 1))
            kb.op("act", lambda e: e.activation(out=self.MODROW[0:2, pc * 512:(pc + 1) * 512], in_=self.PS[pm][0:2, 0:512], func=AF.Copy),
                  reads=[self.pst[pm]], writes=[self.modrowtr])
        for j in range(48):
            kb.op("pe", lambda e: e.transpose(psv[:, j, :], self.MODROW[0:2, j * 128:(j + 1) * 128], self.ID2[0:2, 0:2]),
                  reads=[self.modrowtr, self.tr_small], writes=[self.pst[pb]], signal=(j == 47))
        MOD, DV, NG = self.MOD, self.DV, self.NG
        mtr = self.tr_mod[l]
        for w in range(2):
            kb.op("dve", lambda e: e.tensor_tensor(out=MOD[:, l, w, :], in0=psv[:, :, w], in1=self.ADAB[:, l, :], op=ALU.add),
                  reads=[self.pst[pb], self.tr_small], writes=[mtr])
            kb.op("dve", lambda e: e.scalar_tensor_tensor(out=DV[:, l, w, 0, :], in0=MOD[:, l, w, 8:16], scalar=1.0,
                                                          in1=NG[:, l, 0, :], op0=ALU.add, op1=ALU.mult),
                  reads=[mtr, self.tr_small], writes=[mtr])
            kb.op("dve", lambda e: e.tensor_tensor(out=DV[:, l, w, 1, :], in0=MOD[:, l, w, 16:24], in1=NG[:, l, 1, :], op=ALU.mult),
                  reads=[mtr, self.tr_small], writes=[mtr])
            kb.op("dve", lambda e: e.scalar_tensor_tensor(out=DV[:, l, w, 2, :], in0=MOD[:, l, w, 32:40], scalar=1.0,
                                                          in1=NG[:, l, 2, :], op0=ALU.add, op1=ALU.mult),
                  reads=[mtr, self.tr_small], writes=[mtr])
            kb.op("dve", lambda e: e.tensor_tensor(out=DV[:, l, w, 3, :], in0=MOD[:, l, w, 40:48], in1=NG[:, l, 3, :], op=ALU.mult),
                  reads=[mtr, self.tr_small], writes=[mtr])

    def stats_finish(self, pb, n, rstd=None, rstdtr=None):
        kb = self.kb
        if pb == 6:
            self.flush_stat()
        if rstd is None:
            rstd, rstdtr = self.RSTD, self.rstdtr
        kb.op("act", lambda e: e.activation(out=self.STMP[:, 0:n], in_=self.PS[pb][:, 0:n], func=AF.Ln,
                                            scale=1.0 / D, bias=self.EPSC[:, 0:1]),
              reads=[self.pst[pb], self.ctr_const], writes=[self.stmptr])
        kb.op("act", lambda e: e.activation(out=rstd[:, 0:n], in_=self.STMP[:, 0:n], func=AF.Exp, scale=-0.5),
              reads=[self.stmptr], writes=[rstdtr])

    def prenorm_stats(self, src, strs, c0, n, pb=6, rstd=None, rstdtr=None):
        kb = self.kb
        for kc in range(KC):
            s = self.rot("sq", 3)
            kb.op("act", lambda e: e.activation(out=self.SQ[s][:, 0:n], in_=src[:, kc, c0:c0 + n], func=AF.Square),
                  reads=strs, writes=[self.sqtr[s]])
            kb.op("pe", lambda e: e.matmul(self.PS[pb][:, 0:n], self.ONES[:], self.SQ[s][:, 0:n],
                                           start=(kc == 0), stop=(kc == KC - 1)),
                  reads=[self.sqtr[s], self.ctr_const], writes=[self.pst[pb]], signal=True)
        self.stats_finish(pb, n, rstd, rstdtr)

    def prenorm_apply(self, src, strs, c0, n, l, w, which, rstd=None, rstdtr=None):
        kb = self.kb
        if rstd is None:
            rstd, rstdtr = self.RSTD, self.rstdtr
        ai = 0 if which == 0 else 2
        bj = 0 if which == 0 else 24
        for kc in range(KC):
            s = self.rot("t1", 2)
            kb.op("dve", lambda e: e.tensor_tensor(out=self.T1[s][:, 0:n], in0=src[:, kc, c0:c0 + n], in1=rstd[:, 0:n], op=ALU.mult),
                  reads=strs + [rstdtr], writes=[self.t1tr[s]])
            kb.op("act", lambda e: e.activation(out=self.HT[:, kc, 0:n], in_=self.T1[s][:, 0:n], func=AF.Identity,
                                                scale=self.DV[:, l, w, ai, kc:kc + 1], bias=self.MOD[:, l, w, bj + kc:bj + kc + 1]),
                  reads=[self.t1tr[s], self.tr_mod[l]], writes=[self.httr])

    def prenorm(self, src, strs, c0, n, l, w, which):
        self.prenorm_stats(src, strs, c0, n)
        self.prenorm_apply(src, strs, c0, n, l, w, which)

    def postnorm_residual(self, dst, dtrs, c0, n, l, w, gi):
        kb = self.kb
        self.stats_finish(6, n)
        for kc in range(KC):
            s = self.rot("t1", 2)
            kb.op("dve", lambda e: e.scalar_tensor_tensor(out=self.T1[s][:, 0:n], in0=self.Y[:, kc, 0:n],
                                                          scalar=self.DV[:, l, w, gi, kc:kc + 1], in1=self.RSTD[:, 0:n],
                                                          op0=ALU.mult, op1=ALU.mult),
                  reads=[self.ytr[kc], self.rstdtr, self.tr_mod[l]], writes=[self.t1tr[s]])
            kb.op("pool", lambda e: e.tensor_tensor(out=dst[:, kc, c0:c0 + n], in0=dst[:, kc, c0:c0 + n], in1=self.T1[s][:, 0:n], op=ALU.add),
                  reads=[self.t1tr[s]] + dtrs, writes=dtrs)

    def y_evac(self, pb, kc, n):
        kb = self.kb
        kb.op("act", lambda e: e.activation(out=self.Y[:, kc, 0:n], in_=self.PS[pb][:, 0:n], func=AF.Copy),
              reads=[self.pst[pb]], writes=[self.ytr[kc]])
        s = self.rot("sq", 3)
        kb.op("act", lambda e: e.activation(out=self.SQ[s][:, 0:n], in_=self.PS[pb][:, 0:n], func=AF.Square),
              reads=[self.pst[pb]], writes=[self.sqtr[s]])
        self.flush_stat()

        def stat():
            kb.op("pe", lambda e: e.matmul(self.PS[6][:, 0:n], self.ONES[:], self.SQ[s][:, 0:n], start=(kc == 0), stop=(kc == KC - 1)),
                  reads=[self.sqtr[s], self.ctr_const], writes=[self.pst[6]], signal=True)
        self.pending_stat = stat

    pending_stat = None

    def flush_stat(self):
        if self.pending_stat is not None:
            f = self.pending_stat
            self.pending_stat = None
            f()

    def ffn(self, l):
        kb = self.kb
        if not self.cast_done.get("wup"):
            self.issue_casts(("wup", "wdn"))
        TF = self.TF_L[l]
        self.phase("ffn%d" % l)
        with_ctx = (l == 0 and "ctx" in self.phases)
        if l == 0:
            self.alloc_ct()
        unalias = (l == 1)
        if unalias:
            self.HT = self.aalloc([128, KC, TF], BF16)
        r1 = self.aoff
        self.aoff += 32 * TF
        assert self.aoff <= self.atop, "arena overflow"
        if not unalias:
            self.HT = self.aview(r1, [128, KC, TF], BF16)
        self.httr = Tr("ht")
        ab0 = r1 if unalias else r1 + 16 * TF
        AB = [self.aview(ab0 + i * 4 * TF, [128, TF], F32) for i in range(4)]
        abtr = trs("ab", 4)
        hl = [] if unalias else [self.httr]
        self.Y = self.aview(r1, [128, KC, TF], F32)
        self.ytr = trs("y", KC)
        self.SQ = [self.aalloc([128, TF], BF16) for i in range(3)]
        self.sqtr = trs("sq", 3)
        self.T1 = [self.aalloc([128, TF], F32) for i in range(2)]
        self.t1tr = trs("t1", 2)
        self.STMP = self.aalloc([128, TF], F32)
        self.stmptr = Tr("stmp")
        self.RSTD = self.aalloc([128, TF], F32)
        self.rstdtr = Tr("rstd")
        GT = self.aalloc([128, FC, TF], BF16)
        gttr = trs("gt", FC)
        WUP = [self.aalloc([128, 2, KC, 128], BF16) for i in range(3)]
        wuptr = trs("wup", 3)
        WDN = [self.aalloc([128, FC, 128], BF16) for i in range(2)]
        wdntr = trs("wdn", 4)
        fcw = self.FCW
        RSTDP = self.aalloc([128, TF], F32)
        rstdptr = Tr("rstdp")
        seqs = [(self.XT, L, 0)]
        if with_ctx:
            seqs.append((self.CT, LC, 1))
        alltiles = []
        for (src, n_tok, w) in seqs:
            for (t0, n) in seq_tiles(n_tok, TF)[:self.max_tiles]:
                strs_ = [self.ctr] if w == 1 else self.xt_tracks(t0, t0 + n + 2)
                alltiles.append((src, n_tok, w, t0, n, strs_))
        if True:
            for ti, (src, n_tok, w, t0, n, strs) in enumerate(alltiles):
                ctx = (w == 1)
                if l == 0 and ti == 1:
                    self.issue_casts(("wqk", "wv", "wo"))
                N = n + 2
                c0 = t0
                if ti == 0:
                    self.prenorm_stats(src, strs, c0, N, 7, RSTDP, rstdptr)
                if ti == 0 or not unalias:
                    self.prenorm_apply(src, strs, c0, N, l, w, 1, RSTDP, rstdptr)
                if t0 == 0:
                    kb.op("pool", lambda e: e.memset(self.HT[:, :, 0:1], 0.0), writes=[self.httr])
                else:
                    kb.op("pool", lambda e: e.tensor_copy(out=self.HT[:, :, 0:1], in_=self.HSAVE[:]),
                          reads=[self.hsavetr], writes=[self.httr])
                kb.op("pool", lambda e: e.tensor_copy(out=self.HSAVE[:], in_=self.HT[:, :, N - 2:N - 1]),
                      reads=[self.httr], writes=[self.hsavetr])
                if t0 + n == n_tok:
                    kb.op("pool", lambda e: e.memset(self.HT[:, :, N - 1:N], 0.0), writes=[self.httr])
                for f in range(FC):
                    s = self.rot("wup", 3)
                    wu = WUP[s]
                    kb.dma("sp", wu.rearrange("p a k m -> p (a k m)"), self.d_wup[l, f], reads=[self.tr_wup], writes=[wuptr[s]])
                    pa = self.rot("psA", 2)
                    pbk = 2 + self.rot("psB", 2)
                    for kc in range(KC):
                        kb.op("pe", lambda e: e.matmul(self.PS[pa][:, 0:N], wu[:, 0, kc, :], self.HT[:, kc, 0:N],
                                                       start=(kc == 0), stop=(kc == KC - 1)),
                              reads=[wuptr[s], self.httr], writes=[self.pst[pa]], signal=(kc == KC - 1))
                    for kc in range(KC):
                        kb.op("pe", lambda e: e.matmul(self.PS[pbk][:, 0:n], wu[:, 1, kc, :], self.HT[:, kc, 1:N - 1],
                                                       start=(kc == 0), stop=(kc == KC - 1)),
                              reads=[wuptr[s], self.httr], writes=[self.pst[pbk]], signal=(kc == KC - 1))
                    a0 = self.rot("ab", 4)
                    a1 = self.rot("ab", 4)
                    kb.op("act", lambda e: e.activation(out=AB[a0][:, 0:n], in_=self.PS[pa][:, 1:N - 1], func=AF.Identity,
                                                        scale=fcw[:, l, 1, f:f + 1], bias=fcw[:, l, 3, f:f + 1]),
                          reads=[self.pst[pa], self.tr_small], writes=[abtr[a0]])
                    kb.op("dve", lambda e: e.scalar_tensor_tensor(out=AB[a1][:, 0:n], in0=self.PS[pa][:, 0:n], scalar=fcw[:, l, 0, f:f + 1],
                                                                  in1=AB[a0][:, 0:n], op0=ALU.mult, op1=ALU.add),
                          reads=[self.pst[pa], abtr[a0], self.tr_small], writes=[abtr[a1]])
                    kb.op("dve", lambda e: e.scalar_tensor_tensor(out=AB[a0][:, 0:n], in0=self.PS[pa][:, 2:N], scalar=fcw[:, l, 2, f:f + 1],
                                                                  in1=AB[a1][:, 0:n], op0=ALU.mult, op1=ALU.add),
                          reads=[self.pst[pa], abtr[a1], self.tr_small], writes=[abtr[a0]])
                    kb.op("act", lambda e: e.activation(out=AB[a1][:, 0:n], in_=AB[a0][:, 0:n], func=AF.Gelu_apprx_tanh),
                          reads=[abtr[a0]], writes=[abtr[a1]])
                    kb.op("dve", lambda e: e.tensor_tensor(out=GT[:, f, 0:n], in0=AB[a1][:, 0:n], in1=self.PS[pbk][:, 0:n], op=ALU.mult),
                          reads=[abtr[a1], self.pst[pbk]], writes=[gttr[f]])
                if ti + 1 < len(alltiles):
                    (src2, _, _, t02, n2, strs2) = alltiles[ti + 1]
                    self.prenorm_stats(src2, strs2, t02, n2 + 2, 7, RSTDP, rstdptr)
                    if unalias:
                        self.prenorm_apply(src2, strs2, t02, n2 + 2, l, alltiles[ti + 1][2], 1, RSTDP, rstdptr)
                kb.link(hl + abtr, self.ytr)
                FH = FC // 2
                for j in range(KC):
                    s = self.rot("wdn", 2)
                    wd = WDN[s]
                    wdf = wd.rearrange("p f m -> p (f m)")
                    kb.dma("sp", wdf[:, 0:FH * 128], self.d_wdn[l, j][:, 0:FH * 128], reads=[self.tr_wdn], writes=[wdntr[2 * s]])
                    kb.dma("sp", wdf[:, FH * 128:FC * 128], self.d_wdn[l, j][:, FH * 128:FC * 128], reads=[self.tr_wdn], writes=[wdntr[2 * s + 1]])
                    py = 4 + self.rot("psY", 2)
                    for f in range(FC):
                        kb.op("pe", lambda e: e.matmul(self.PS[py][:, 0:n], wd[:, f, :], GT[:, f, 0:n],
                                                       start=(f == 0), stop=(f == FC - 1)),
                              reads=[wdntr[2 * s + (0 if f < FH else 1)], gttr[f]], writes=[self.pst[py]],
                              signal=(f == FC - 1 or f == FH - 1))
                    self.y_evac(py, j, n)
                dtrs = [self.ctr] if ctx else self.xt_tracks(c0 + 1, c0 + 1 + n)
                self.postnorm_residual(src, dtrs, c0 + 1, n, l, w, 3)
                kb.link(self.ytr, hl + abtr)
                if (not ctx) and (l == 1 or ("stop_after_ffn0" in self.phases)):
                    self.store(t0, n)

    def store(self, t0, n):
        ev = self.kb.dma("sp", self.d_out[:, :, t0:t0 + n], self.XT[:, :, 1 + t0:1 + t0 + n],
                         reads=self.xt_tracks(1 + t0, 1 + t0 + n))
        self.out_evs.append(ev)

    def epilogue(self):
        kb = self.kb
        sp = kb.eng["sp"]
        deps = {}
        for s, v in self.out_evs:
            if deps.get(s, 0) < v:
                deps[s] = v
        waits = kb._waits(sp, deps)

        def run(e, waits=waits):
            for s, v in waits:
                e.wait_ge(s.h, v)

        sp.prog.append(run)

    def mixer_a(self, l):
        kb = self.kb
        TM = self.TMA
        self.phase("mix0")
        self.alloc_ct()
        self.HT = self.aalloc([128, KC, TM], BF16)
        self.httr = Tr("ht")
        self.SQ = [self.aalloc([128, TM], BF16) for i in range(3)]
        self.sqtr = trs("sq", 3)
        self.T1 = [self.aalloc([128, TM], F32) for i in range(2)]
        self.t1tr = trs("t1", 2)
        self.STMP = self.aalloc([128, TM], F32)
        self.stmptr = Tr("stmp")
        self.RSTD = self.aalloc([128, TM], F32)
        self.rstdtr = Tr("rstd")
        UT = self.aalloc([128, EC, TM], BF16)
        uttr = trs("ut", EC)
        rv = self.aoff
        self.aoff += 32 * TM
        VT = self.aview(rv, [128, EC, TM], BF16)
        vttr = trs("vt", EC)
        self.Y = self.aview(rv, [128, KC, TM], F32)
        self.ytr = trs("y", KC)
        WUP = [self.aalloc([128, 2, KC, 128], BF16) for i in range(3)]
        whtr = trs("wh", 6)
        WOUT = [self.aalloc([128, EC, 128], BF16) for i in range(2)]
        wouttr = trs("wout", 2)
        MEAN = self.aalloc([128, TM], F32)
        meantr = Tr("mean")
        MSQ, msqtr = self.STMP, self.stmptr
        RSTD2, rstd2tr = self.RSTD, self.rstdtr
        VTOK = [self.aalloc([128, TM // 128, 128], BF16) for i in range(2)]
        vtoktr = trs("vtok", 2)
        WST = self.aalloc([128, 8, 128], BF16)
        BS = self.aalloc([128, 8, 128], BF16)
        IDENT = self.aalloc([128, 128], BF16)
        ctr2 = Tr("const2")
        kb.dma("pool", WST.rearrange("p g m -> p (g m)"), self.d_wst, writes=[ctr2])
        kb.dma("pool", BS[0:1].rearrange("p g m -> p (g m)"), self.d_bs, writes=[ctr2])
        kb.dma("pool", IDENT, self.d_ident, writes=[ctr2])
        RSTDP = self.aalloc([128, TM], F32)
        rstdptr = Tr("rstdp")
        seqs = [(self.XT, L, 0)]
        if "ctx" in self.phases:
            seqs.append((self.CT, LC, 1))
        alltiles = []
        for (src, n_tok, w) in seqs:
            for t0 in [t0 for t0 in range(0, n_tok, TM)][:self.max_tiles]:
                n_ = min(TM, n_tok - t0)
                strs_ = [self.ctr] if w == 1 else self.xt_tracks(1 + t0, 1 + t0 + n_)
                alltiles.append((src, n_tok, w, t0, n_, strs_))
        if True:
            for ti, (src, n_tok, w, t0, n, strs) in enumerate(alltiles):
                ctx = (w == 1)
                if (not ctx) and t0 == TM:
                    self.issue_casts(("wup", "wdn"))
                NCH = n // 128
                c0 = 1 + t0
                if ti == 0:
                    self.prenorm_stats(src, strs, c0, n, 6, RSTDP, rstdptr)
                    self.prenorm_apply(src, strs, c0, n, l, w, 0, RSTDP, rstdptr)
                WH = [WUP[i // 2][:, i % 2] for i in range(6)]
                pend_ln = None
                for f in range(EC):
                    s = self.rot("wh", 6)
                    kb.dma("sp", WH[s].rearrange("p k m -> p (k m)"), self.d_win[f][:, KC * 128:2 * KC * 128],
                           reads=[self.tr_win], writes=[whtr[s]])
                    pbk = 2 + self.rot("psB", 2)
                    for kc in range(KC):
                        kb.op("pe", lambda e: e.matmul(self.PS[pbk][:, 0:n], WH[s][:, kc, :], self.HT[:, kc, 0:n],
                                                       start=(kc == 0), stop=(kc == KC - 1)),
                              reads=[whtr[s], self.httr], writes=[self.pst[pbk]], signal=(kc == KC - 1))
                    kb.op("act", lambda e: e.activation(out=VT[:, f, 0:n], in_=self.PS[pbk][:, 0:n], func=AF.Gelu_apprx_tanh),
                          reads=[self.pst[pbk]], writes=[vttr[f]])
                    sq = self.rot("sq", 3)
                    kb.op("act", lambda e: e.activation(out=self.SQ[sq][:, 0:n], in_=VT[:, f, 0:n], func=AF.Square),
                          reads=[vttr[f]], writes=[self.sqtr[sq]])
                    if pend_ln is not None:
                        pend_ln()

                    def ln_stats(f=f, sq=sq):
                        kb.op("pe", lambda e: e.matmul(self.PS[6][:, 0:n], self.ONES[:], VT[:, f, 0:n], start=(f == 0), stop=(f == EC - 1)),
                              reads=[vttr[f], self.ctr_const], writes=[self.pst[6]], signal=True)
                        kb.op("pe", lambda e: e.matmul(self.PS[7][:, 0:n], self.ONES[:], self.SQ[sq][:, 0:n], start=(f == 0), stop=(f == EC - 1)),
                              reads=[self.sqtr[sq], self.ctr_const], writes=[self.pst[7]], signal=True)
                    pend_ln = ln_stats
                pend_ln()

                def stage_u(f):
                    s = self.rot("wh", 6)
                    kb.dma("sp", WH[s].rearrange("p k m -> p (k m)"), self.d_win[f][:, 0:KC * 128],
                           reads=[self.tr_win], writes=[whtr[s]])
                    pa = self.rot("psA", 2)
                    for kc in range(KC):
                        kb.op("pe", lambda e: e.matmul(self.PS[pa][:, 0:n], WH[s][:, kc, :], self.HT[:, kc, 0:n],
                                                       start=(kc == 0), stop=(kc == KC - 1)),
                              reads=[whtr[s], self.httr], writes=[self.pst[pa]], signal=(kc == KC - 1))
                    kb.op("act", lambda e: e.activation(out=UT[:, f, 0:n], in_=self.PS[pa][:, 0:n], func=AF.Gelu_apprx_tanh),
                          reads=[self.pst[pa]], writes=[uttr[f]])
                kb.op("dve", lambda e: e.tensor_scalar(out=MEAN[:, 0:n], in0=self.PS[6][:, 0:n], scalar1=1.0 / E, scalar2=None, op0=ALU.mult),
                      reads=[self.pst[6]], writes=[meantr])
                kb.op("dve", lambda e: e.tensor_tensor(out=MSQ[:, 0:n], in0=MEAN[:, 0:n], in1=MEAN[:, 0:n], op=ALU.mult),
                      reads=[meantr], writes=[msqtr])
                kb.op("dve", lambda e: e.scalar_tensor_tensor(out=MSQ[:, 0:n], in0=self.PS[7][:, 0:n], scalar=1.0 / E, in1=MSQ[:, 0:n],
                                                              op0=ALU.mult, op1=ALU.subtract),
                      reads=[self.pst[7], msqtr], writes=[msqtr])
                kb.op("act", lambda e: e.activation(out=MSQ[:, 0:n], in_=MSQ[:, 0:n], func=AF.Ln, scale=1.0, bias=self.EPSC[:, 0:1]),
                      reads=[msqtr, self.ctr_const], writes=[msqtr])
                kb.op("act", lambda e: e.activation(out=RSTD2[:, 0:n], in_=MSQ[:, 0:n], func=AF.Exp, scale=-0.5),
                      reads=[msqtr], writes=[rstd2tr])
                st = {}

                def stage_n(f):
                    t1 = self.rot("t1", 2)
                    kb.op("dve", lambda e: e.tensor_tensor(out=self.T1[t1][:, 0:n], in0=VT[:, f, 0:n], in1=MEAN[:, 0:n], op=ALU.subtract),
                          reads=[vttr[f], meantr], writes=[self.t1tr[t1]])
                    kb.op("dve", lambda e: e.scalar_tensor_tensor(out=self.T1[t1][:, 0:n], in0=self.T1[t1][:, 0:n],
                                                                  scalar=self.LNGB[:, 0, f:f + 1], in1=RSTD2[:, 0:n],
                                                                  op0=ALU.mult, op1=ALU.mult),
                          reads=[self.t1tr[t1], rstd2tr, self.tr_small], writes=[self.t1tr[t1]])
                    kb.op("act", lambda e: e.activation(out=VT[:, f, 0:n], in_=self.T1[t1][:, 0:n], func=AF.Identity,
                                                        bias=self.LNGB[:, 1, f:f + 1], scale=1.0),
                          reads=[self.t1tr[t1], self.tr_small], writes=[vttr[f]])

                def stage_t(f):
                    ptb = 4 + self.rot("psY", 2)
                    ptv = self.PS[ptb][:, :].bitcast(BF16)
                    vk = self.rot("vtok", 2)
                    st[f] = vk
                    for c in range(NCH):
                        kb.op("pe", lambda e: e.transpose(ptv[:, c * 128:(c + 1) * 128], VT[:, f, c * 128:(c + 1) * 128], IDENT),
                              reads=[vttr[f], ctr2], writes=[self.pst[ptb]], signal=(c == NCH - 1))
                    kb.op("act", lambda e: e.activation(out=VTOK[vk].rearrange("p c m -> p (c m)")[:, 0:NCH * 128], in_=ptv[:, 0:NCH * 128], func=AF.Copy),
                          reads=[self.pst[ptb]], writes=[vtoktr[vk]])

                def stage_s(f):
                    g = f // 2
                    vk = st[f]
                    pa = 2 + self.rot("psB", 2)
                    kb.op("pe", lambda e: e.matmul(self.PS[pa][:, 0:NCH * 128], self.ONES[0:1, :],
                                                   BS[0:1, g, :].unsqueeze(1).broadcast_to([1, NCH, 128]), start=True, stop=False),
                          reads=[self.ctr_const, ctr2], writes=[self.pst[pa]], signal=False)
                    for c in range(NCH):
                        kb.op("pe", lambda e: e.matmul(self.PS[pa][:, c * 128:(c + 1) * 128], VTOK[vk][:, c, :], WST[:, g, :],
                                                       start=False, stop=(c == NCH - 1)),
                              reads=[vtoktr[vk], ctr2], writes=[self.pst[pa]], signal=(c == NCH - 1))
                    kb.op("dve", lambda e: e.tensor_tensor(out=UT[:, f, 0:n], in0=UT[:, f, 0:n], in1=self.PS[pa][:, 0:n], op=ALU.mult),
                          reads=[uttr[f], self.pst[pa]], writes=[uttr[f]])

                for step in range(EC + 2):
                    if step < EC:
                        stage_u(step)
                        stage_n(step)
                    if 0 <= step - 1 < EC:
                        stage_t(step - 1)
                    if 0 <= step - 2 < EC:
                        stage_s(step - 2)
                if ti + 1 < len(alltiles):
                    (src2, _, _, t02, n2, strs2) = alltiles[ti + 1]
                    self.prenorm_stats(src2, strs2, 1 + t02, n2, 7, RSTDP, rstdptr)
                    self.prenorm_apply(src2, strs2, 1 + t02, n2, l, alltiles[ti + 1][2], 0, RSTDP, rstdptr)
                kb.link(vttr, self.ytr)
                for j in range(KC):
                    s = self.rot("wout", 2)
                    wd = WOUT[s]
                    kb.dma("sp", wd.rearrange("p f m -> p (f m)"), self.d_wout[j], reads=[self.tr_wout], writes=[wouttr[s]])
                    py = 4 + self.rot("psY", 2)
                    for f in range(EC):
                        kb.op("pe", lambda e: e.matmul(self.PS[py][:, 0:n], wd[:, f, :], UT[:, f, 0:n], start=(f == 0), stop=(f == EC - 1)),
                              reads=[wouttr[s], uttr[f]], writes=[self.pst[py]], signal=(f == EC - 1))
                    self.y_evac(py, j, n)
                dtrs = [self.ctr] if ctx else self.xt_tracks(c0, c0 + n)
                self.postnorm_residual(src, dtrs, c0, n, l, w, 1)
                kb.link(self.ytr, vttr)
                if (not ctx) and "stop_after_mix0" in self.phases:
                    self.store(t0, n)

    def mixer_b(self, l):
        kb = self.kb
        TM = self.TM
        assert TM == 256
        n = TM
        if not self.cast_done.get("wqk"):
            self.issue_casts(("wqk", "wv", "wo"))
        self.phase("attn_ctx")
        KTC = self.aalloc([128, 2, LC], BF16, top=True)
        VC = self.aalloc([128, 2, 4, 128], BF16, top=True)
        kvctr = Tr("kvc")
        self.alloc_ct()
        self.alloc_common(TM)
        WQ = [self.aalloc([128, KC, 128], BF16) for i in range(3)]
        wqtr = trs("wq", 3)
        WV = self.aalloc([128, KC, 256], BF16)
        wvtr = Tr("wv")
        BV = self.aalloc([128, 256], F32)
        bvtr = Tr("bv")
        kb.dma("sp", BV, self.d_bv, writes=[bvtr])
        kb.op("pool", lambda e: e.memset(VC[:, :, :, 64:128], 1.0), writes=[kvctr])
        self.prenorm(self.CT, [self.ctr], 1, LC, l, 1, 0)
        for c in range(2):
            s = self.rot("wq", 3)
            kb.dma("sp", WQ[s].rearrange("p k m -> p (k m)"), self.d_wqk[8 + c], reads=[self.tr_wqk], writes=[wqtr[s]])
            pa = 4 + self.rot("psY", 2)
            for kc in range(KC):
                kb.op("pe", lambda e: e.matmul(self.PS[pa][:, 0:LC], WQ[s][:, kc, :], self.HT[:, kc, 0:LC],
                                               start=(kc == 0), stop=(kc == KC - 1)),
                      reads=[wqtr[s], self.httr], writes=[self.pst[pa]], signal=(kc == KC - 1))
            kb.op("act", lambda e: e.activation(out=KTC[:, c, :], in_=self.PS[pa][:, 0:LC], func=AF.Identity,
                                                bias=self.BQK[:, 8 + c:9 + c], scale=1.0),
                  reads=[self.pst[pa], self.tr_small], writes=[kvctr])
        kb.dma("sp", WV.rearrange("p k m -> p (k m)"), self.d_wv, reads=[self.tr_wv], writes=[wvtr])
        for blk in range(2):
            pa = 4 + self.rot("psY", 2)
            for kc in range(KC):
                kb.op("pe", lambda e: e.matmul(self.PS[pa][:, 0:256], self.HT[:, kc, blk * 128:(blk + 1) * 128], WV[:, kc, :],
                                               start=(kc == 0), stop=(kc == KC - 1)),
                      reads=[wvtr, self.httr], writes=[self.pst[pa]], signal=(kc == KC - 1))
            kb.op("dve", lambda e: e.tensor_tensor(out=VC[:, blk, :, 0:64], in0=self.PS[pa][:, 0:256].rearrange("p (h d) -> p h d", h=4),
                                                   in1=BV.rearrange("p (h d) -> p h d", h=4), op=ALU.add),
                  reads=[self.pst[pa], bvtr], writes=[kvctr])

        self.phase("attn_main")
        KTC = self.aalloc([128, 2, LC], BF16, top=True)
        VC = self.aalloc([128, 2, 4, 128], BF16, top=True)
        kvctr = Tr("kvc")
        self.alloc_common(TM)
        WQ = [self.aalloc([128, KC, 128], BF16) for i in range(3)]
        wqtr = trs("wq", 3)
        WV = self.aalloc([128, KC, 256], BF16)
        wvtr = Tr("wv")
        WO = [self.aalloc([128, KC, 128], BF16) for i in range(2)]
        wotr = trs("wo", 2)
        BV = self.aalloc([128, 256], F32)
        bvtr = Tr("bv")
        MASK = self.aalloc([128, 2, 128], BF16)
        masktr = Tr("mask")
        COS = self.aalloc([128, TM], F32)
        SIN = self.aalloc([128, TM], F32)
        ropetr = Tr("rope")
        QF = [self.aalloc([128, TM], F32) for i in range(2)]
        qftr = trs("qf", 2)
        TQ2 = [self.aalloc([128, TM], F32) for i in range(2)]
        tqtr2 = trs("tq", 2)
        UQ2 = [self.aalloc([128, TM], F32) for i in range(2)]
        uqtr2 = trs("uq", 2)
        QT = [self.aalloc([128, KC, TM], BF16) for i in range(2)]
        qttr = trs("qt", 2)
        KT = self.aalloc([128, 2, RING * 128], BF16)
        kttr = trs("kt", RING)
        VR = self.aalloc([128, RING, 4, 128], BF16)
        vrtr = trs("vr", RING)
        PT = [self.aalloc([128, 512], BF16) for i in range(3)]
        pttr = trs("pt", 3)
        RC = [self.aalloc([128, 512], F32) for i in range(2)]
        rctr = trs("rc", 2)
        OT2 = [self.aalloc([128, KC, TM], BF16) for i in range(2)]
        ottr2 = [trs("ot%d_" % i, KC) for i in range(2)]
        kb.dma("sp", BV, self.d_bv, writes=[bvtr])
        kb.dma("pool", MASK.rearrange("p a q -> p (a q)"), self.d_mask, writes=[masktr])
        kb.op("pool", lambda e: e.memset(VR[:, :, :, 64:128], 1.0), writes=vrtr)

        ntiles = L // TM
        if self.max_tiles is not None:
            ntiles = min(ntiles, self.max_tiles)
        holds = {}

        def stage_a_items(i):
            t0 = i * TM
            c0 = 1 + t0
            qs = i % 2
            items = []

            def head():
                self.prenorm(self.XT, self.xt_tracks(c0, c0 + n), c0, n, l, 0, 0)
                kb.dma("sp", COS, self.d_cos[:, t0:t0 + n], writes=[ropetr])
                kb.dma("sp", SIN, self.d_sin[:, t0:t0 + n], writes=[ropetr])
            items.append(head)
            for c in (8, 9):
                items.append(lambda c=c: proj_chunk(i, qs, c))
            items.append(lambda: v_proj(i))
            for c in range(8):
                items.append(lambda c=c: proj_chunk(i, qs, c))
            return items

        def proj_chunk(i, qs, c):
            if True:
                s = self.rot("wq", 3)
                kb.dma("sp", WQ[s].rearrange("p k m -> p (k m)"), self.d_wqk[c], reads=[self.tr_wqk], writes=[wqtr[s]])
                pa = 4 + self.rot("psY", 2)
                for kc in range(KC):
                    kb.op("pe", lambda e: e.matmul(self.PS[pa][:, 0:n], WQ[s][:, kc, :], self.HT[:, kc, 0:n],
                                                   start=(kc == 0), stop=(kc == KC - 1)),
                          reads=[wqtr[s], self.httr], writes=[self.pst[pa]], signal=(kc == KC - 1))
                qf = self.rot("qf", 2)
                TQ, tqtr, UQ, uqtr = TQ2[qf], tqtr2[qf], UQ2[qf], uqtr2[qf]
                kb.op("act", lambda e: e.activation(out=QF[qf], in_=self.PS[pa][:, 0:n], func=AF.Identity,
                                                    bias=self.BQK[:, c:c + 1], scale=1.0),
                      reads=[self.pst[pa], self.tr_small], writes=[qftr[qf]])
                kb.op("pool", lambda e: e.tensor_tensor(out=TQ, in0=QF[qf], in1=COS, op=ALU.mult),
                      reads=[qftr[qf], ropetr], writes=[tqtr])
                for q4 in range(4):
                    sp_ = q4 * 32
                    dp = sp_ ^ 32
                    kb.op("dve", lambda e: e.tensor_tensor(out=UQ[dp:dp + 32, :], in0=QF[qf][sp_:sp_ + 32, :], in1=SIN[sp_:sp_ + 32, :],
                                                           op=ALU.mult),
                          reads=[qftr[qf], ropetr], writes=[uqtr])
                if c < 8:
                    kb.op("pool", lambda e: e.tensor_tensor(out=QT[qs][:, c, :], in0=TQ, in1=UQ, op=ALU.add),
                          reads=[tqtr, uqtr], writes=[qttr[qs]])
                else:
                    for blk in range(2):
                        jb = 2 * i + blk
                        sl = jb % RING
                        kb.op("pool", lambda e: e.tensor_tensor(out=KT[:, c - 8, sl * 128:(sl + 1) * 128], in0=TQ[:, blk * 128:(blk + 1) * 128],
                                                                in1=UQ[:, blk * 128:(blk + 1) * 128], op=ALU.add),
                              reads=[tqtr, uqtr], writes=[kttr[sl]])
                        holds[("k", c - 8, sl)] = jb
        def v_proj(i):
            if i == 0:
                kb.dma("sp", WV.rearrange("p k m -> p (k m)"), self.d_wv, reads=[self.tr_wv], writes=[wvtr])
            for blk in range(2):
                jb = 2 * i + blk
                sl = jb % RING
                pa = 4 + self.rot("psY", 2)
                for kc in range(KC):
                    kb.op("pe", lambda e: e.matmul(self.PS[pa][:, 0:256], self.HT[:, kc, blk * 128:(blk + 1) * 128], WV[:, kc, :],
                                                   start=(kc == 0), stop=(kc == KC - 1)),
                          reads=[wvtr, self.httr], writes=[self.pst[pa]], signal=(kc == KC - 1))
                kb.op("dve", lambda e: e.tensor_tensor(out=VR[:, sl, :, 0:64], in0=self.PS[pa][:, 0:256].rearrange("p (h d) -> p h d", h=4),
                                                       in1=BV.rearrange("p (h d) -> p h d", h=4), op=ALU.add),
                      reads=[self.pst[pa], bvtr], writes=[vrtr[sl]])
                holds[("v", sl)] = jb

        def stage_b(i, inter):
            qs = i % 2
            OT, ottr = OT2[i % 2], ottr2[i % 2]
            jobs = []
            for blk in range(2):
                jb = 2 * i + blk
                for kvh in range(4):
                    pr, half = kvh // 2, kvh % 2
                    p0 = half * 64
                    rhs_q = QT[qs][p0:p0 + 64, pr * 4:(pr + 1) * 4, blk * 128:(blk + 1) * 128]
                    chunks = []
                    if jb > 0:
                        chunks.append(("lat", jb - 1, 0))
                    chunks.append(("lat", jb, None))
                    if jb < NBLK - 1:
                        chunks.append(("lat", jb + 1, 1))
                    chunks.append(("ctx", 0, None))
                    chunks.append(("ctx", 1, None))
                    pv = 2 + self.rot("psB", 2)
                    for ci, (kind, kbk, mk) in enumerate(chunks):
                        if kind == "lat":
                            sl = kbk % RING
                            job = dict(lhs_k=KT[p0:p0 + 64, pr, sl * 128:(sl + 1) * 128], ktr=kttr[sl],
                                       lhs_v=VR[:, sl, kvh, :], vtr=vrtr[sl], need=(kbk, sl))
                        else:
                            job = dict(lhs_k=KTC[p0:p0 + 64, pr, kbk * 128:(kbk + 1) * 128], ktr=kvctr,
                                       lhs_v=VC[:, kbk, kvh, :], vtr=kvctr)
                        job.update(mk=mk, first=(ci == 0), last=(ci == len(chunks) - 1), pv=pv, rhs_q=rhs_q,
                                   kvh=kvh, blk=blk, pr=pr, p0=p0)
                        jobs.append(job)

            def emit_s(jb_):
                if "need" in jb_:
                    kbk_, sl_ = jb_["need"]
                    assert holds.get(("k", jb_["pr"], sl_)) == kbk_ and holds.get(("v", sl_)) == kbk_, "ring slot not yet produced"
                pa = (0, 1, 7)[self.rot("psS", 3)]
                ps_ = self.rot("pt", 3)
                jb_["pt"] = ps_
                kb.op("pe", lambda e: e.matmul(self.PS[pa][:, 0:512], jb_["lhs_k"], jb_["rhs_q"], start=True, stop=True),
                      reads=[jb_["ktr"], qttr[qs]], writes=[self.pst[pa]], signal=True)
                kb.op("act", lambda e: e.activation(out=PT[ps_], in_=self.PS[pa][:, 0:512], func=AF.Exp, scale=0.125),
                      reads=[self.pst[pa]], writes=[pttr[ps_]])
                if jb_["mk"] is not None:
                    kb.op("dve", lambda e: e.tensor_tensor(out=PT[ps_].rearrange("p (g q) -> p g q", g=4),
                                                           in0=PT[ps_].rearrange("p (g q) -> p g q", g=4),
                                                           in1=MASK[:, jb_["mk"], :].unsqueeze(1).broadcast_to([128, 4, 128]), op=ALU.mult),
                          reads=[pttr[ps_], masktr], writes=[pttr[ps_]])

            def emit_pv(jb_):
                pv = jb_["pv"]
                ps_ = jb_["pt"]
                kb.op("pe", lambda e: e.matmul(self.PS[pv][:, 0:512], jb_["lhs_v"], PT[ps_], start=jb_["first"], stop=jb_["last"]),
                      reads=[jb_["vtr"], pttr[ps_]], writes=[self.pst[pv]], signal=True)
                if not jb_["last"]:
                    return
                kvh, blk, pr, p0 = jb_["kvh"], jb_["blk"], jb_["pr"], jb_["p0"]
                rc = self.rot("rc", 2)
                for g in range(4):
                    h = kvh * 4 + g
                    kb.op("act", lambda e: e.activation(out=RC[rc][0:64, g * 128:(g + 1) * 128], in_=self.PS[pv][64:128, g * 128:(g + 1) * 128],
                                                        func=AF.Ln, bias=self.SE[64:128, h:h + 1], scale=1.0),
                          reads=[self.pst[pv], self.tr_small], writes=[rctr[rc]])
                kb.op("act", lambda e: e.activation(out=RC[rc][0:64, :], in_=RC[rc][0:64, :], func=AF.Exp, scale=-1.0),
                      reads=[rctr[rc]], writes=[rctr[rc]])
                kb.op("dve", lambda e: e.tensor_tensor(out=OT[p0:p0 + 64, pr * 4:(pr + 1) * 4, blk * 128:(blk + 1) * 128],
                                                       in0=self.PS[pv][0:64, 0:512].rearrange("p (g q) -> p g q", g=4),
                                                       in1=RC[rc][0:64, :].rearrange("p (g q) -> p g q", g=4), op=ALU.mult),
                      reads=[self.pst[pv], rctr[rc]], writes=ottr[pr * 4:(pr + 1) * 4])

            SK = 2
            for idx in range(len(jobs) + SK):
                if idx < len(jobs):
                    emit_s(jobs[idx])
                if 0 <= idx - SK < len(jobs):
                    emit_pv(jobs[idx - SK])
                if inter:
                    inter.pop(0)()
            while inter:
                inter.pop(0)()

        def stage_c_items(i):
            t0 = i * TM
            c0 = 1 + t0
            OT, ottr = OT2[i % 2], ottr2[i % 2]
            items = []

            def wo_group(j):
                s = self.rot("wo", 2)
                kb.dma("sp", WO[s].rearrange("p k m -> p (k m)"), self.d_wo[j], reads=[self.tr_wo], writes=[wotr[s]])
                py = 4 + self.rot("psY", 2)
                for c in range(KC):
                    kb.op("pe", lambda e: e.matmul(self.PS[py][:, 0:n], WO[s][:, c, :], OT[:, c, :], start=(c == 0), stop=(c == KC - 1)),
                          reads=[wotr[s], ottr[c]], writes=[self.pst[py]], signal=(c == KC - 1))
                self.y_evac(py, j, n)

            def post_kc(kc):
                dtrs = self.xt_tracks(c0, c0 + n)
                kb.op("dve", lambda e: e.scalar_tensor_tensor(out=self.Y[:, kc, 0:n], in0=self.Y[:, kc, 0:n],
                                                              scalar=self.DV[:, l, 0, 1, kc:kc + 1], in1=self.RSTD[:, 0:n],
                                                              op0=ALU.mult, op1=ALU.mult),
                      reads=[self.rstdtr, self.tr_mod[l]], writes=[self.ytr[kc]])
                kb.op("pool", lambda e: e.tensor_tensor(out=self.XT[:, kc, c0:c0 + n], in0=self.XT[:, kc, c0:c0 + n], in1=self.Y[:, kc, 0:n], op=ALU.add),
                      reads=[self.ytr[kc]] + dtrs, writes=dtrs)

            for j in range(KC):
                items.append(lambda j=j: wo_group(j))
            items.append(lambda: self.stats_finish(6, n))
            for kc in range(KC):
                items.append(lambda kc=kc: post_kc(kc))
            if "stop_after_mix1" in self.phases:
                items.append(lambda: self.store(t0, n))
            return items

        for it in stage_a_items(0):
            it()
        for i in range(ntiles):
            inter = stage_c_items(i - 1) if i >= 1 else []
            if i + 1 < L // TM:
                inter = inter + stage_a_items(i + 1)
            stage_b(i, inter)
        for it in stage_c_items(ntiles - 1):
            it()


def _qperm():
    cols = []
    for c in range(8):
        pr, g = c // 4, c % 4
        for hh in (2 * pr, 2 * pr + 1):
            h = hh * 4 + g
            cols.extend(range(h * 64, (h + 1) * 64))
    return np.array(cols)


def prep_shared(inp):
    f32 = np.float32
    out = {}
    ada_w = np.asarray(inp["ada_w"], f32)
    out["adaw"] = np.ascontiguousarray(ada_w.reshape(2, KC, 128, 12, 512).transpose(0, 3, 2, 1, 4))
    out["id2"] = np.eye(2, dtype=f32)
    ada_b = np.asarray(inp["ada_b"], f32)
    out["adab"] = np.ascontiguousarray(ada_b.reshape(2, 48, 128).transpose(2, 0, 1))
    ng = np.asarray(inp["norm_g"], f32)
    out["ng"] = np.ascontiguousarray(ng.reshape(2, 4, KC, 128).transpose(3, 0, 1, 2))
    wup = np.asarray(inp["f_w_up"], f32)
    out["wup"] = np.ascontiguousarray(wup.reshape(2, KC, 128, 2, FC, 128).transpose(0, 4, 2, 3, 1, 5)).reshape(2, FC, 128, 2 * KC * 128)
    wdn = np.asarray(inp["f_w_down"], f32)
    out["wdn"] = np.ascontiguousarray(wdn.reshape(2, FC, 128, KC, 128).transpose(0, 3, 2, 1, 4)).reshape(2, KC, 128, FC * 128)
    cw = np.asarray(inp["f_conv_w"], f32)
    cb = np.asarray(inp["f_conv_b"], f32)
    fcw = np.concatenate([cw, cb[:, None, :]], axis=1)
    out["fcw"] = np.ascontiguousarray(fcw.reshape(2, 4, FC, 128).transpose(3, 0, 1, 2))
    win = np.asarray(inp["a_w_in"], f32)[0]
    out["win"] = np.ascontiguousarray(win.reshape(KC, 128, 2, EC, 128).transpose(3, 1, 2, 0, 4)).reshape(EC, 128, 2 * KC * 128)
    wout = np.asarray(inp["a_w_out"], f32)[0]
    out["wout"] = np.ascontiguousarray(wout.reshape(EC, 128, KC, 128).transpose(2, 1, 0, 3)).reshape(KC, 128, EC * 128)
    ws = np.asarray(inp["a_w_s"], f32)[0]
    out["wst"] = np.ascontiguousarray(ws.transpose(2, 0, 1)).reshape(128, 8 * 128)
    out["bs"] = np.ascontiguousarray(np.asarray(inp["a_b_s"], f32)[0]).reshape(1, 8 * 128)
    lngb = np.stack([np.asarray(inp["a_ln_g"], f32)[0], np.asarray(inp["a_ln_b"], f32)[0]], axis=0)
    out["lngb"] = np.ascontiguousarray(lngb.reshape(2, EC, 128).transpose(2, 0, 1))
    out["ident"] = np.eye(128, dtype=f32)
    wqkv = np.asarray(inp["b_w_qkv"], f32)[0]
    bqkv = np.asarray(inp["b_b_qkv"], f32)[0]
    qp = _qperm()
    cols = np.concatenate([qp, np.arange(1024, 1280)])
    wqk = wqkv[:, cols]
    out["wqk"] = np.ascontiguousarray(wqk.reshape(KC, 128, 10, 128).transpose(2, 1, 0, 3)).reshape(10, 128, KC * 128)
    out["bqk"] = np.ascontiguousarray(bqkv[cols].reshape(10, 128).T)
    wv = wqkv[:, 1280:1536]
    out["wv"] = np.ascontiguousarray(wv.reshape(KC, 128, 256).transpose(1, 0, 2)).reshape(128, KC * 256)
    out["bv"] = np.ascontiguousarray(np.broadcast_to(bqkv[1280:1536][None, :], (128, 256)))
    wo = np.asarray(inp["b_w_o"], f32)[0][qp, :]
    out["wo"] = np.ascontiguousarray(wo.reshape(KC, 128, KC, 128).transpose(2, 1, 0, 3)).reshape(KC, 128, KC * 128)
    out["sink"] = np.ascontiguousarray(np.broadcast_to(np.asarray(inp["b_sink"], f32)[0][None, :], (128, 16)))
    t = np.arange(L)
    row = (t // 64).astype(f32)
    col = (t % 64).astype(f32)
    inv_freq = (f32(10000.0) ** (-(np.arange(16, dtype=f32) / f32(16)))).astype(f32)
    ang = np.concatenate([row[:, None] * inv_freq, col[:, None] * inv_freq], axis=-1).astype(f32)
    cosT = np.cos(ang).astype(f32).T
    sinT = np.sin(ang).astype(f32).T
    p = np.arange(128)
    out["ropec"] = np.ascontiguousarray(cosT[p % 32])
    sgn = np.where((p % 64) >= 32, -1.0, 1.0).astype(f32)
    out["ropes"] = np.ascontiguousarray(sinT[p % 32] * sgn[:, None])
    kl = np.arange(128)[:, None]
    ql = np.arange(128)[None, :]
    out["masks"] = np.ascontiguousarray(np.stack([(kl >= ql), (kl <= ql)], axis=1).astype(f32)).reshape(128, 256)
    return out


def prep_core(inp, b):
    f32 = np.float32
    x = np.asarray(inp["x"][b], f32)
    ctx = np.asarray(inp["ctx"][b], f32)
    c = np.asarray(inp["c"][b], f32)
    cctx = np.asarray(inp["c_ctx"], f32)
    m = {}
    m["xT"] = np.ascontiguousarray(x.reshape(L, KC, 128).transpose(2, 1, 0))
    m["ctxT"] = np.ascontiguousarray(ctx.reshape(LC, KC, 128).transpose(2, 1, 0))
    cc = np.stack([c, cctx], axis=-1)
    m["cc"] = np.ascontiguousarray(cc.reshape(KC, 128, 2).transpose(1, 0, 2))
    return m


def run_prog(inp, phases, cores=NCORES, **kw):
    p = Prog(phases, **kw)
    nc = p.build()
    shared = prep_shared(inp)
    in_maps = []
    for b in range(cores):
        m = prep_core(inp, b)
        m.update(shared)
        in_maps.append(m)
    res = run_bass_kernel_spmd(nc, in_maps, core_ids=list(range(cores)))
    outs = []
    for b in range(cores):
        o = res.results[b]["outT"]
        outs.append(np.ascontiguousarray(o.transpose(2, 1, 0)).reshape(L, D))
    return np.stack(outs, axis=0)


def kernel(**inputs):
    out = run_prog(inputs, {"mix0", "ffn0", "ctx", "mix1", "ffn1"})
    return out.astype(np.float32)
```
